# Optimizing a Trainium2 kernel written in Bass

```python
import functools
import jax, jax.numpy as jnp
from jax import lax
import numpy as np

D_MODEL = 1024
BATCH = 16
SEQ = 2048
DEPTH = 4

GRID_W = 64
CTX_LEN = 256
N_MIXERS = 2
N_A = (DEPTH + 1) // 2
N_B = DEPTH // 2
N_MOD = 9
D_FF = 2816
CONV_DIM = D_MODEL
CONV_W = 3
MLA_HEADS = 8
QK_NOPE = 128
QK_ROPE = 64
QK_HEAD = QK_NOPE + QK_ROPE
V_HEAD = 128
Q_LORA = 256
KV_LORA = 128
ROPE_BASE = 10000.0
QK_SCALE = QK_HEAD ** -0.5
Q_BLOCK = 128
EPS = 1e-6

kernel_name = "hybrid_shortconv_mla_macaron_dit"


def rms_norm(x, g):
    xf = x.astype(jnp.float32)
    y = xf * lax.rsqrt(jnp.mean(xf * xf, axis=-1, keepdims=True) + EPS)
    return (y * g.astype(jnp.float32)).astype(x.dtype)


def adaln_chunks(cond, w_mod, b_mod):
    m = jax.nn.silu(cond) @ w_mod + b_mod
    return jnp.split(m[:, None, :], N_MOD, axis=-1)


def pre(h, g, shift, scale):
    return rms_norm(h, g) * (1 + scale) + shift


def swiglu(h, w1, w3, w2):
    return (jax.nn.silu(h @ w1) * (h @ w3)) @ w2


def conv3_centred(u, w):
    return lax.conv_general_dilated(
        u, w[:, None, :].astype(u.dtype), window_strides=(1,), padding=((1, 1),),
        dimension_numbers=("NWC", "WIO", "NWC"), feature_group_count=u.shape[-1])


def short_conv_mixer(h, w_in, conv_w, w_out):
    b_gate, c_gate, u = jnp.split(h @ w_in, 3, axis=-1)
    return (b_gate * conv3_centred(c_gate * u, conv_w)) @ w_out


def axial_rope_tables(n):
    rows = n // GRID_W
    r = jnp.broadcast_to(jnp.arange(rows)[:, None], (rows, GRID_W)).reshape(n).astype(jnp.float32)
    col = jnp.broadcast_to(jnp.arange(GRID_W)[None, :], (rows, GRID_W)).reshape(n).astype(jnp.float32)
    n_freq = QK_ROPE // 4
    inv = ROPE_BASE ** (-jnp.arange(n_freq, dtype=jnp.float32) / n_freq)
    ang = jnp.stack([r[:, None] * inv, col[:, None] * inv], axis=1)
    return jnp.cos(ang), jnp.sin(ang)


def apply_axial_rope(t, cos, sin):
    ts = t.reshape(t.shape[:-1] + (2, 2, QK_ROPE // 4))
    x1, x2 = ts[..., 0, :], ts[..., 1, :]
    cos = cos.astype(t.dtype)
    sin = sin.astype(t.dtype)
    y = jnp.stack([x1 * cos - x2 * sin, x1 * sin + x2 * cos], axis=-2)
    return y.reshape(t.shape)


def rope_tail(t, cos, sin):
    return jnp.concatenate([t[..., :QK_NOPE], apply_axial_rope(t[..., QK_NOPE:], cos, sin)], axis=-1)


def mla_down(h, w_a):
    return jnp.split(h @ w_a, [Q_LORA, Q_LORA + KV_LORA], axis=-1)


def mla_queries(cq, g_qa, w_uq, g_q):
    b, n, _ = cq.shape
    q = (rms_norm(cq, g_qa) @ w_uq).reshape(b, n, MLA_HEADS, QK_HEAD)
    return rms_norm(q, g_q).transpose(0, 2, 1, 3)


def mla_keys_values(ckv, k_rope, g_kva, w_ukv, g_k):
    b, n, _ = ckv.shape
    kv = (rms_norm(ckv, g_kva) @ w_ukv).reshape(b, n, MLA_HEADS, QK_NOPE + V_HEAD)
    k_nope, v = jnp.split(kv, [QK_NOPE], axis=-1)
    k_r = jnp.broadcast_to(k_rope[:, :, None, :], (b, n, MLA_HEADS, QK_ROPE))
    k = rms_norm(jnp.concatenate([k_nope, k_r], axis=-1), g_k)
    return k.transpose(0, 2, 1, 3), v.transpose(0, 2, 1, 3)


def softmax_attend(q, k, v):
    s = jnp.einsum("bhqd,bhkd->bhqk", q, k).astype(jnp.float32) * QK_SCALE
    p = jax.nn.softmax(s, axis=-1).astype(v.dtype)
    return jnp.einsum("bhqk,bhkd->bhqd", p, v)


def merge_heads(o):
    b, h, n, d = o.shape
    return o.transpose(0, 2, 1, 3).reshape(b, n, h * d)


def latent_attention(q, k_all, v_all):
    b, h, n, dq = q.shape
    nb = n // Q_BLOCK
    qb = q.reshape(b, h, nb, Q_BLOCK, dq).transpose(2, 0, 1, 3, 4)
    o = lax.map(lambda qblk: softmax_attend(qblk, k_all, v_all), qb)
    return o.transpose(1, 0, 3, 2, 4).reshape(b, n, h * V_HEAD)


def setup_inputs(seed: int = 0) -> dict:
    key = jax.random.key(seed)
    ks = jax.random.split(key, 24)
    f32 = jnp.float32

    def nrm(k, shape, scale):
        return jax.random.normal(k, shape, f32) * scale

    def gain(k, shape):
        return 1.0 + 0.1 * jax.random.normal(k, shape, f32)

    D = D_MODEL
    return {
        "x": nrm(ks[0], (BATCH, SEQ, D), 1.0),
        "c": nrm(ks[1], (BATCH, D), 1.0),
        "ctx": nrm(ks[2], (BATCH, CTX_LEN, D), 1.0),
        "c_ctx": nrm(ks[3], (D,), 1.0),
        "w_mod": nrm(ks[4], (DEPTH, D, N_MOD * D), 0.5 * D ** -0.5),
        "b_mod": nrm(ks[5], (DEPTH, N_MOD * D), 0.02),
        "g_norm": gain(ks[6], (DEPTH, 3, D)),
        "ffn_w1": nrm(ks[7], (DEPTH, 2, D, D_FF), D ** -0.5),
        "ffn_w3": nrm(ks[8], (DEPTH, 2, D, D_FF), D ** -0.5),
        "ffn_w2": nrm(ks[9], (DEPTH, 2, D_FF, D), D_FF ** -0.5),
        "sc_w_in": nrm(ks[10], (N_A, D, 3 * CONV_DIM), D ** -0.5),
        "sc_conv": nrm(ks[11], (N_A, CONV_W, CONV_DIM), CONV_W ** -0.5),
        "sc_w_out": nrm(ks[12], (N_A, CONV_DIM, D), CONV_DIM ** -0.5),
        "mla_w_a": nrm(ks[13], (N_B, D, Q_LORA + KV_LORA + QK_ROPE), D ** -0.5),
        "mla_g_qa": gain(ks[14], (N_B, Q_LORA)),
        "mla_w_uq": nrm(ks[15], (N_B, Q_LORA, MLA_HEADS * QK_HEAD), Q_LORA ** -0.5),
        "mla_g_kva": gain(ks[16], (N_B, KV_LORA)),
        "mla_w_ukv": nrm(ks[17], (N_B, KV_LORA, MLA_HEADS * (QK_NOPE + V_HEAD)), KV_LORA ** -0.5),
        "mla_g_q": gain(ks[18], (N_B, QK_HEAD)),
        "mla_g_k": gain(ks[19], (N_B, QK_HEAD)),
        "mla_w_o": nrm(ks[20], (N_B, MLA_HEADS * V_HEAD, D), (MLA_HEADS * V_HEAD) ** -0.5),
    }


def reference(x, c, ctx, c_ctx, w_mod, b_mod, g_norm, ffn_w1, ffn_w3, ffn_w2,
              sc_w_in, sc_conv, sc_w_out, mla_w_a, mla_g_qa, mla_w_uq, mla_g_kva,
              mla_w_ukv, mla_g_q, mla_g_k, mla_w_o):
    n = x.shape[1]
    cos, sin = axial_rope_tables(n)
    h_x, h_c = x, ctx
    for i in range(DEPTH):
        kind, j = i % N_MIXERS, i // N_MIXERS
        last = i == DEPTH - 1
        run_ctx_in = (not last) or kind == 1
        run_ctx_out = not last

        mx = adaln_chunks(c, w_mod[i], b_mod[i])
        mc = adaln_chunks(c_ctx[None], w_mod[i], b_mod[i])
        ffn1 = functools.partial(swiglu, w1=ffn_w1[i, 0], w3=ffn_w3[i, 0], w2=ffn_w2[i, 0])
        ffn2 = functools.partial(swiglu, w1=ffn_w1[i, 1], w3=ffn_w3[i, 1], w2=ffn_w2[i, 1])

        h_x = h_x + 0.5 * mx[2] * ffn1(pre(h_x, g_norm[i, 0], mx[0], mx[1]))
        if run_ctx_in:
            h_c = h_c + 0.5 * mc[2] * ffn1(pre(h_c, g_norm[i, 0], mc[0], mc[1]))

        nx = pre(h_x, g_norm[i, 1], mx[3], mx[4])
        if kind == 0:
            ox = short_conv_mixer(nx, sc_w_in[j], sc_conv[j], sc_w_out[j])
            if run_ctx_out:
                nc = pre(h_c, g_norm[i, 1], mc[3], mc[4])
                oc = short_conv_mixer(nc, sc_w_in[j], sc_conv[j], sc_w_out[j])
        else:
            nc = pre(h_c, g_norm[i, 1], mc[3], mc[4])
            cq_c, ckv_c, kr_c = mla_down(nc, mla_w_a[j])
            k_c, v_c = mla_keys_values(ckv_c, kr_c, mla_g_kva[j], mla_w_ukv[j], mla_g_k[j])
            cq_x, ckv_x, kr_x = mla_down(nx, mla_w_a[j])
            k_x, v_x = mla_keys_values(ckv_x, kr_x, mla_g_kva[j], mla_w_ukv[j], mla_g_k[j])
            k_x = rope_tail(k_x, cos, sin)
            q_x = rope_tail(mla_queries(cq_x, mla_g_qa[j], mla_w_uq[j], mla_g_q[j]), cos, sin)
            k_all = jnp.concatenate([k_c, k_x], axis=2)
            v_all = jnp.concatenate([v_c, v_x], axis=2)
            ox = latent_attention(q_x, k_all, v_all) @ mla_w_o[j]
            if run_ctx_out:
                q_c = mla_queries(cq_c, mla_g_qa[j], mla_w_uq[j], mla_g_q[j])
                oc = merge_heads(softmax_attend(q_c, k_c, v_c)) @ mla_w_o[j]
        h_x = h_x + mx[5] * ox
        if run_ctx_out:
            h_c = h_c + mc[5] * oc

        h_x = h_x + 0.5 * mx[8] * ffn2(pre(h_x, g_norm[i, 2], mx[6], mx[7]))
        if run_ctx_out:
            h_c = h_c + 0.5 * mc[8] * ffn2(pre(h_c, g_norm[i, 2], mc[6], mc[7]))
    return h_x
```

```python
import os
import math
from contextlib import ExitStack
import numpy as np
import concourse.bass as bass
import concourse.mybir as mybir
from concourse.bass_utils import run_bass_kernel_spmd

F32 = mybir.dt.float32
BF16 = mybir.dt.bfloat16
ALU = mybir.AluOpType
AF = mybir.ActivationFunctionType

ENGS = ["pe", "act", "dve", "pool", "sp"]
NDMA_SEM = 12


class Res:
    __slots__ = ("name", "w", "r", "rd", "scr")

    def __init__(self, name="", scr=False):
        self.name = name
        self.w = None
        self.r = {}
        self.rd = []
        self.scr = scr


class Op:
    __slots__ = ("eng", "fn", "deps", "key", "val", "signalled", "is_dma", "waits", "snap")


class Sched:
    def __init__(self):
        self.ops = []
        self.per_eng = {e: [] for e in ENGS}
        self.ndma = {"sp": 0, "pool": 0, "act": 0}
        self.dma_ops = {"sp": [], "pool": [], "act": []}

    def emit(self, eng, fn, reads=(), writes=(), dma=False):
        op = Op()
        op.eng = eng
        op.fn = fn
        op.is_dma = dma
        op.signalled = dma
        deps = set()
        for r in reads:
            if r.w is not None:
                deps.add(r.w)
        for w in writes:
            if w.w is not None:
                deps.add(w.w)
            deps.update(w.r.values())
            deps.update(w.rd)
        if dma:
            i = self.ndma[eng]
            self.ndma[eng] += 1
            op.key = ("dma", eng, i % NDMA_SEM)
            op.val = 16 * (i // NDMA_SEM + 1)
            if i >= NDMA_SEM:
                deps.add(self.dma_ops[eng][i - NDMA_SEM])
            self.dma_ops[eng].append(op)
        else:
            op.key = eng
            op.val = None
        if eng == "pe" and not dma:
            deps = {d for d in deps if d.is_dma or d.eng != "pe"}
        op.deps = deps
        for d in deps:
            d.signalled = True
        for r in reads:
            if dma:
                r.rd.append(op)
            else:
                r.r[eng] = op
        for w in writes:
            w.w = op
            w.r = {}
            w.rd = []
        self.ops.append(op)
        self.per_eng[eng].append(op)
        return op

    def finalize(self):
        counts = {e: 0 for e in ENGS}
        known = {e: {} for e in ENGS}
        for op in self.ops:
            kn = known[op.eng]
            waits = {}
            for d in op.deps:
                if kn.get(d.key, 0) >= d.val:
                    continue
                if waits.get(d.key, 0) < d.val:
                    waits[d.key] = d.val
            for d in op.deps:
                for k, v in d.snap.items():
                    if kn.get(k, 0) < v:
                        kn[k] = v
            op.waits = waits
            if not op.is_dma and op.signalled:
                counts[op.eng] += 1
                op.val = counts[op.eng]
            if op.signalled:
                snap = dict(kn)
                if snap.get(op.key, 0) < op.val:
                    snap[op.key] = op.val
                op.snap = snap
            else:
                op.snap = None
            op.deps = None
        self.counts = counts

    def run(self, block, sems):
        per_eng = self.per_eng

        def replay(ename, eng):
            for op in per_eng[ename]:
                for k, v in op.waits.items():
                    eng.wait_ge(sems[k], v)
                ins = op.fn(eng)
                if op.signalled:
                    ins.then_inc(sems[op.key], 16 if op.is_dma else 1)

        @block.tensor
        def _(eng):
            replay("pe", eng)

        @block.scalar
        def _(eng):
            replay("act", eng)

        @block.vector
        def _(eng):
            replay("dve", eng)

        @block.gpsimd
        def _(eng):
            replay("pool", eng)
            last = {}
            for op in self.dma_ops["pool"]:
                last[op.key] = op.val
            for k, v in last.items():
                eng.wait_ge(sems[k], v)

        @block.sync
        def _(eng):
            replay("sp", eng)
            last = {}
            for op in self.dma_ops["sp"]:
                last[op.key] = op.val
            for k, v in last.items():
                eng.wait_ge(sems[k], v)


D = 1024
NX = 2048
NCX = 256
NTOK = NX + NCX
DFF = 2816
NFF = DFF // 128
DEPTH = 4
H = 8
EPS = 1e-6
CHUNKS = [("x", 0, 512, 0), ("x", 512, 512, 512), ("x", 1024, 512, 1024), ("x", 1536, 512, 1536),
          ("c", 0, 256, 2048)]
FF_GROUPS = [(0, 6), (6, 14), (14, 22)]

OFF_GN, OFF_BM, OFF_CV, OFF_GQA, OFF_GKVA, OFF_GQ, OFF_GK, OFF_EPS, NG = 0, 96, 384, 432, 436, 438, 444, 450, 452
WSLOT = 2048
NSLOT = 6
SCR_BYTES = 53248


class Builder:
    def __init__(self, n_layers=DEPTH, nb=2):
        self.n_layers = n_layers
        self.nb = nb
        self.S = Sched()
        self.nc = bass.Bass("TRN2", target_bir_lowering=False)

    def E(self, eng, method, reads, writes, *args, **kw):
        if any(r.scr for r in reads) or any(w.scr for w in writes):
            reads = list(reads) + [self.RE]
        return self.S.emit(eng, lambda e: getattr(e, method)(*args, **kw), reads, writes)

    def DMA(self, q, out, in_, reads, writes):
        if any(r.scr for r in reads) or any(w.scr for w in writes):
            reads = list(reads) + [self.RE]
        return self.S.emit(q, lambda e: e.dma_start(out=out, in_=in_), reads, writes, dma=True)

    def psum(self):
        i = self.ps_i
        self.ps_i = (i + 1) % 6
        return self.ps[i], self.Rps[i]

    def tf(self):
        i = self.tf_i
        self.tf_i = (i + 1) % len(self.TF)
        return self.TF[i], self.RTF[i]

    def tb(self):
        i = self.tb_i
        self.tb_i = (i + 1) % len(self.TB)
        return self.TB[i], self.RTB[i]

    def fence(self):
        self.E("dve", "memset", [], [self.RE, self.Rdummy], self.dummy[:, 0:1], 0.0)
        self.scr_off = 0

    def salloc(self, nelem, dt):
        bpe = 2 if dt == BF16 else 4
        off = (self.scr_off + 3) // 4 * 4
        nbytes = nelem * bpe
        assert off + nbytes <= SCR_BYTES, (off, nbytes)
        self.scr_off = off + nbytes
        v = self.scr[:, off // 2: (off + nbytes) // 2]
        if dt != BF16:
            v = v.bitcast(dt)
        return v

    def wget(self, src, shape):
        i = self.w_i
        self.w_i += 1
        s = i % NSLOT
        n = int(np.prod(shape[1:]))
        assert n <= WSLOT
        v = self.wring[s][:, 0:n]
        if len(shape) == 3:
            v = v.rearrange("p (a b) -> p a b", a=shape[1])
        self.DMA("pool", v, src, [], [self.Rw[s]])
        return v, self.Rw[s]

    def build(self):
        nc = self.nc
        nb = self.nb
        dr = lambda name, shape: nc.dram_tensor(name, shape, F32, kind="ExternalInput").ap()
        self.xT = dr("xT", [nb, D, NX])
        self.cT = dr("cT", [nb, D, NCX])
        self.condT = dr("condT", [D, 3])
        self.gpack_d = dr("gpack", [128, NG])
        self.rope_d = dr("rope", [64, 2 * NX])
        self.w_mod = dr("w_mod", [DEPTH, D, 9 * D])
        self.w1 = dr("ffn_w1", [DEPTH, 2, D, DFF])
        self.w3 = dr("ffn_w3", [DEPTH, 2, D, DFF])
        self.w2 = dr("ffn_w2", [DEPTH, 2, DFF, D])
        self.sc_w_in = dr("sc_w_in", [2, D, 3 * D])
        self.sc_w_out = dr("sc_w_out", [2, D, D])
        self.w_a = dr("w_a", [2, D, 512])
        self.w_uqn = dr("w_uqn", [2, 256, H * 128])
        self.w_uqr = dr("w_uqr", [2, 256, H * 128])
        self.w_ukv = dr("w_ukv", [2, 128, 2048])
        self.w_o = dr("w_o", [2, D, D])
        self.yT = nc.dram_tensor("yT", [nb, D, NX], F32, kind="ExternalOutput").ap()
        self.dbg = os.environ.get("KTEST_DBG", "")
        if self.dbg:
            self.dbgM = nc.dram_tensor("dbgM", [128, DEPTH * 72 * 3], F32, kind="ExternalOutput").ap()
            self.dbgN = nc.dram_tensor("dbgN", [128, 8 * NTOK], F32, kind="ExternalOutput").ap()

        with ExitStack() as es:
            sb = lambda name, shape, dt: es.enter_context(nc.sbuf_tensor(name, shape, dt))
            self.hx = sb("hx", [128, 8, NX], F32)
            self.hc = sb("hc", [128, 8, NCX], F32)
            self.nT = sb("nT", [128, 8, NTOK], BF16)
            self.wring = [sb(f"wr{i}", [128, WSLOT], BF16) for i in range(NSLOT)]
            self.M = sb("M", [128, DEPTH, 72, 3], F32)
            self.DRV = sb("DRV", [128, DEPTH, 5, 8, 3], F32)
            self.gp = sb("gp", [128, NG], F32)
            self.ones = sb("ones_t", [128, 128], BF16)
            self.TF = [sb(f"tf{i}", [128, 512], F32) for i in range(4)]
            self.TB = [sb(f"tb{i}", [128, 512], BF16) for i in range(4)]
            self.TR = [sb(f"tr{i}", [128, 512], F32) for i in range(2)]
            self.RTR = [Res(f"tr{i}") for i in range(2)]
            self.tr_i = 0
            self.cond = sb("cond_t", [128, 8, 3], F32)
            self.scond = sb("scond", [128, 8, 3], BF16)
            self.dummy = sb("dummy_t", [128, 4], F32)
            self.scr = sb("scr", [128, SCR_BYTES // 2], BF16)
            self.ps = [es.enter_context(nc.psum_tensor(f"ps{i}", [128, 512], F32)) for i in range(8)]
            sems = {e: es.enter_context(nc.semaphore("s_" + e)) for e in ENGS}
            for q in ("sp", "pool"):
                for j in range(NDMA_SEM):
                    sems[("dma", q, j)] = es.enter_context(nc.semaphore(f"d_{q}{j}"))
            block = es.enter_context(nc.Block())

            self.Rh = {}
            for ci, (s, t0, n, col) in enumerate(CHUNKS):
                for kc in range(8):
                    self.Rh[(ci, kc)] = Res(f"h{ci}_{kc}")
            self.RnT = {(ci, kc): Res(f"n{ci}_{kc}") for ci in range(5) for kc in range(8)}
            self.Rw = [Res(f"w{i}") for i in range(NSLOT)]
            self.Rps = [Res(f"ps{i}") for i in range(8)]
            self.RTF = [Res(f"tf{i}") for i in range(4)]
            self.RTB = [Res(f"tb{i}") for i in range(4)]
            self.RM = [Res(f"M{l}") for l in range(DEPTH)]
            self.RDRV = [Res(f"DRV{l}") for l in range(DEPTH)]
            self.Rgp = Res("gp")
            self.Rones = Res("ones")
            self.Rcond = Res("cond")
            self.Rscond = Res("scond")
            self.RE = Res("epoch")
            self.Rdummy = Res("dummy")
            self.ps_i = self.tf_i = self.tb_i = self.w_i = 0
            self.mq = None
            self.scr_off = 0

            self.prologue()
            for bi in range(nb):
                self.load_h(bi)
                self.run_batch(bi)
                self.store_h(bi)

            self.S.finalize()
            self.S.run(block, sems)
        return nc

    def hview(self, ci, kc):
        s, t0, n, col = CHUNKS[ci]
        return (self.hx if s == "x" else self.hc)[:, kc, t0:t0 + n]

    def prologue(self):
        self.DMA("sp", self.gp[:], self.gpack_d, [], [self.Rgp])
        self.DMA("sp", self.cond[:], self.condT.rearrange("(kc p) v -> p kc v", p=128), [], [self.Rcond])
        self.E("dve", "memset", [], [self.Rones], self.ones[:], 1.0)
        self.E("act", "activation", [self.Rcond], [self.Rscond], out=self.scond[:], in_=self.cond[:], func=AF.Silu)
        self.fence()
        NST = 3
        stage = [self.salloc(8 * 512, BF16).rearrange("p (a b) -> p a b", a=8) for _ in range(NST)]
        Rst = [Res(f"st{i}", True) for i in range(NST)]
        si = 0
        for l in range(1):
            pM, RpM = self.psum()
            wv = self.w_mod[l].rearrange("(kc p) n -> p kc n", p=128)
            for n0 in range(0, 9 * D, 512):
                st, Rs = stage[si % NST], Rst[si % NST]
                si += 1
                self.DMA("pool", st, wv[:, :, n0:n0 + 512], [], [Rs])
                for oc in range(4):
                    j = n0 // 128 + oc
                    for kc in range(8):
                        self.E("pe", "matmul", [Rs, self.Rscond], [RpM], pM[:, 3 * j:3 * j + 3],
                               lhsT=st[:, kc, oc * 128:(oc + 1) * 128], rhs=self.scond[:, kc, :],
                               start=(kc == 0), stop=(kc == 7))
            self.mods_finish(l, pM, RpM)

    def mods_finish(self, l, pM, RpM):
        bm = self.gp[:, OFF_BM + l * 72: OFF_BM + (l + 1) * 72].unsqueeze(2).to_broadcast([128, 72, 3])
        self.E("dve", "tensor_tensor", [RpM, self.Rgp], [self.RM[l]], out=self.M[:, l, :, :],
               in0=pM[:, 0:216].rearrange("p (j v) -> p j v", v=3), in1=bm, op=ALU.add)
        for k in range(3):
            sc_ = self.M[:, l, (3 * k + 1) * 8:(3 * k + 2) * 8, :]
            g = self.gp[:, OFF_GN + (l * 3 + k) * 8: OFF_GN + (l * 3 + k + 1) * 8].unsqueeze(2).to_broadcast([128, 8, 3])
            self.E("dve", "scalar_tensor_tensor", [self.RM[l], self.Rgp], [self.RDRV[l]],
                   out=self.DRV[:, l, k, :, :], in0=sc_, scalar=1.0, in1=g, op0=ALU.add, op1=ALU.mult)
        for k, which in ((3, 2), (4, 8)):
            self.E("dve", "tensor_scalar", [self.RM[l]], [self.RDRV[l]], out=self.DRV[:, l, k, :, :],
                   in0=self.M[:, l, which * 8:(which + 1) * 8, :], scalar1=0.5, scalar2=None, op0=ALU.mult)

    def mods_begin(self, l):
        self.mq = {"l": l, "issued": 0, "done": 0,
                   "st": [(self.salloc(8 * 128, BF16).rearrange("p (a b) -> p a b", a=8), Res(f"mst{i}", True))
                          for i in range(3)]}

    def mods_tick(self):
        mq = self.mq
        if mq is None:
            return
        l = mq["l"]
        wv = self.w_mod[l].rearrange("(kc p) n -> p kc n", p=128)
        while mq["issued"] < 72 and mq["issued"] < mq["done"] + 3:
            jj = mq["issued"]
            st, Rs = mq["st"][jj % 3]
            self.DMA("pool", st, wv[:, :, jj * 128:(jj + 1) * 128], [], [Rs])
            mq["issued"] += 1
        jj = mq["done"]
        st, Rs = mq["st"][jj % 3]
        pM, RpM = self.ps[6], self.Rps[6]
        for kc in range(8):
            self.E("pe", "matmul", [Rs, self.Rscond], [RpM], pM[:, 3 * jj:3 * jj + 3], lhsT=st[:, kc, :],
                   rhs=self.scond[:, kc, :], start=(kc == 0), stop=(kc == 7))
        mq["done"] += 1
        if mq["done"] == 72:
            self.mods_finish(l, pM, RpM)
            self.mq = None

    def load_h(self, bi):
        for kc in range(8):
            self.DMA("sp", self.hx[:, kc, :], self.xT[bi, kc * 128:(kc + 1) * 128, :], [],
                     [self.Rh[(ci, kc)] for ci in range(4)])
        self.DMA("sp", self.hc[:], self.cT[bi].rearrange("(kc p) n -> p kc n", p=128), [],
                 [self.Rh[(4, kc)] for kc in range(8)])

    def store_h(self, bi):
        for kc in range(8):
            self.DMA("sp", self.yT[bi, kc * 128:(kc + 1) * 128, :], self.hx[:, kc, :],
                     [self.Rh[(ci, kc)] for ci in range(4)], [])

    def norm_sq(self, bi, ci, sqb):
        s, t0, n, col = CHUNKS[ci]
        for kc in range(8):
            sq, Rsq = sqb[kc]
            self.E("act", "activation", [self.Rh[(ci, kc)]], [Rsq], out=sq[:, :n], in_=self.hview(ci, kc),
                   func=AF.Square)

    def norm_rest(self, l, k, bi, ci, sqb):
        epsap = self.gp[:, OFF_EPS:OFF_EPS + 1]
        s, t0, n, col = CHUNKS[ci]
        v = bi if s == "x" else 2
        pss, Rpss = self.psum()
        for kc in range(8):
            sq, Rsq = sqb[kc]
            self.E("pe", "matmul", [Rsq, self.Rones], [Rpss], pss[:, :n], lhsT=self.ones[:], rhs=sq[:, :n],
                   start=(kc == 0), stop=(kc == 7))
        rstd, Rrstd = self.TR[self.tr_i], self.RTR[self.tr_i]
        self.tr_i = 1 - self.tr_i
        self.E("act", "activation", [Rpss, self.Rgp], [Rrstd], out=rstd[:, :n], in_=pss[:, :n], func=AF.Sqrt,
               scale=1.0 / D, bias=epsap)
        self.E("dve", "reciprocal", [Rrstd], [Rrstd], out=rstd[:, :n], in_=rstd[:, :n])
        for kc in range(8):
            tmp, Rtmp = self.tf()
            self.E("dve", "scalar_tensor_tensor", [self.Rh[(ci, kc)], self.RDRV[l], Rrstd], [Rtmp],
                   out=tmp[:, :n], in0=self.hview(ci, kc), scalar=self.DRV[:, l, k, kc, v:v + 1],
                   in1=rstd[:, :n], op0=ALU.mult, op1=ALU.mult)
            self.E("act", "activation", [Rtmp, self.RM[l]], [self.RnT[(ci, kc)]],
                   out=self.nT[:, kc, col:col + n], in_=tmp[:, :n], func=AF.Identity,
                   bias=self.M[:, l, (3 * k) * 8 + kc, v:v + 1], scale=1.0)

    def norm(self, l, k, bi, cis):
        for ci in cis:
            if ci in self.pre:
                continue
            s, t0, n, col = CHUNKS[ci]
            pss, Rpss = self.psum()
            for kc in range(8):
                sq, Rsq = self.tb()
                if kc % 2 == 0:
                    self.E("act", "activation", [self.Rh[(ci, kc)]], [Rsq], out=sq[:, :n], in_=self.hview(ci, kc),
                           func=AF.Square)
                else:
                    self.E("dve", "tensor_tensor", [self.Rh[(ci, kc)]], [Rsq], out=sq[:, :n],
                           in0=self.hview(ci, kc), in1=self.hview(ci, kc), op=ALU.mult)
                self.E("pe", "matmul", [Rsq, self.Rones], [Rpss], pss[:, :n], lhsT=self.ones[:], rhs=sq[:, :n],
                       start=(kc == 0), stop=(kc == 7))
            self._norm_tail(l, k, bi, ci, pss, Rpss)

    def _norm_tail(self, l, k, bi, ci, pss, Rpss):
        epsap = self.gp[:, OFF_EPS:OFF_EPS + 1]
        s, t0, n, col = CHUNKS[ci]
        v = bi if s == "x" else 2
        rstd, Rrstd = self.TR[self.tr_i], self.RTR[self.tr_i]
        self.tr_i = 1 - self.tr_i
        self.E("act", "activation", [Rpss, self.Rgp], [Rrstd], out=rstd[:, :n], in_=pss[:, :n], func=AF.Sqrt,
               scale=1.0 / D, bias=epsap)
        self.E("dve", "reciprocal", [Rrstd], [Rrstd], out=rstd[:, :n], in_=rstd[:, :n])
        for kc in range(8):
            tmp, Rtmp = self.tf()
            self.E("dve", "scalar_tensor_tensor", [self.Rh[(ci, kc)], self.RDRV[l], Rrstd], [Rtmp],
                   out=tmp[:, :n], in0=self.hview(ci, kc), scalar=self.DRV[:, l, k, kc, v:v + 1],
                   in1=rstd[:, :n], op0=ALU.mult, op1=ALU.mult)
            self.E("act", "activation", [Rtmp, self.RM[l]], [self.RnT[(ci, kc)]],
                   out=self.nT[:, kc, col:col + n], in_=tmp[:, :n], func=AF.Identity,
                   bias=self.M[:, l, (3 * k) * 8 + kc, v:v + 1], scale=1.0)

    def alloc_nsq(self):
        self.nsq = [(self.salloc(512, BF16), Res(f"nsq{i}", True)) for i in range(8)]

    def cb_final(self, bi, ci):
        if self.nxt is None:
            return
        l2, k2, cis2 = self.nxt
        if self.pending is not None:
            self.norm_rest(l2, k2, bi, self.pending, self.nsq)
            self.pending = None
        if ci in cis2:
            self.norm_sq(bi, ci, self.nsq)
            self.pending = ci
            self.pre_next.add(ci)

    def cb_flush(self, bi):
        if self.nxt is not None and self.pending is not None:
            l2, k2, cis2 = self.nxt
            self.norm_rest(l2, k2, bi, self.pending, self.nsq)
            self.pending = None

    def ffn(self, l, f, bi, cis):
        k = 0 if f == 0 else 2
        self.norm(l, k, bi, cis)
        if self.dbg and l == 0 and bi == 0 and f == 0:
            self.DMA("sp", self.dbgM[:, 0:216], self.M[:, 0].rearrange("p b c -> p (b c)"), list(self.RM), [])
            self.DMA("pool", self.dbgN, self.nT[:].rearrange("p a b -> p (a b)"), list(self.RnT.values()), [])
        self.fence()
        GL = 8
        act = self.salloc(GL * NTOK, BF16).rearrange("p (a b) -> p a b", a=GL)
        self.alloc_nsq()
        Ract = {(a, ci): Res(f"act{a}_{ci}", True) for a in range(GL) for ci in range(5)}
        if bi == 0 and f == 0 and l + 1 < self.n_layers:
            self.mods_begin(l + 1)
        w1v = self.w1[l, f].rearrange("(kc p) n -> p kc n", p=128)
        w3v = self.w3[l, f].rearrange("(kc p) n -> p kc n", p=128)
        w2v = self.w2[l, f].rearrange("(fc p) n -> p fc n", p=128)
        gk = 3 if f == 0 else 4
        def get_pair(p0):
            return (self.wget(w1v[:, :, p0 * 128:(p0 + 2) * 128], [128, 8, 256]),
                    self.wget(w3v[:, :, p0 * 128:(p0 + 2) * 128], [128, 8, 256]))

        cur = None
        for gi, (g0, g1) in enumerate(FF_GROUPS):
            pairs = list(range(g0, g1, 2))
            if cur is None:
                cur = get_pair(pairs[0])
            us = []
            for pidx, p0 in enumerate(pairs):
                if pidx + 1 < len(pairs):
                    nxt = get_pair(pairs[pidx + 1])
                else:
                    nxt = None
                    for q0 in pairs:
                        us.append(self.wget(w2v[:, q0:q0 + 2, :], [128, 2, 1024]))
                (u1, R1), (u3, R3) = cur
                cur = nxt
                for fc in range(p0, p0 + 2):
                    a = fc - g0
                    c0 = (fc - p0) * 128
                    for ci in cis:
                        s, t0, n, col = CHUNKS[ci]
                        p1, Rp1 = self.psum()
                        p3, Rp3 = self.psum()
                        for kc in range(8):
                            self.E("pe", "matmul", [R1, self.RnT[(ci, kc)]], [Rp1], p1[:, :n],
                                   lhsT=u1[:, kc, c0:c0 + 128], rhs=self.nT[:, kc, col:col + n],
                                   start=(kc == 0), stop=(kc == 7))
                        for kc in range(8):
                            self.E("pe", "matmul", [R3, self.RnT[(ci, kc)]], [Rp3], p3[:, :n],
                                   lhsT=u3[:, kc, c0:c0 + 128], rhs=self.nT[:, kc, col:col + n],
                                   start=(kc == 0), stop=(kc == 7))
                        sl, Rsl = self.tf()
                        self.E("act", "activation", [Rp1], [Rsl], out=sl[:, :n], in_=p1[:, :n], func=AF.Silu)
                        self.E("dve", "tensor_tensor", [Rsl, Rp3], [Ract[(a, ci)]], out=act[:, a, col:col + n],
                               in0=sl[:, :n], in1=p3[:, :n], op=ALU.mult)
                        self.mods_tick()
            if gi + 1 < len(FF_GROUPS):
                cur = get_pair(FF_GROUPS[gi + 1][0])
            for ci in cis:
                s, t0, n, col = CHUNKS[ci]
                v = bi if s == "x" else 2
                for d in range(8):
                    po, Rpo = self.psum()
                    for fc in range(g0, g1):
                        a = fc - g0
                        u2, R2 = us[a // 2]
                        self.E("pe", "matmul", [R2, Ract[(a, ci)]], [Rpo], po[:, :n],
                               lhsT=u2[:, a % 2, d * 128:(d + 1) * 128], rhs=act[:, a, col:col + n],
                               start=(fc == g0), stop=(fc == g1 - 1))
                    hv = self.hview(ci, d)
                    self.E("dve", "scalar_tensor_tensor", [Rpo, self.RDRV[l], self.Rh[(ci, d)]], [self.Rh[(ci, d)]],
                           out=hv, in0=po[:, :n], scalar=self.DRV[:, l, gk, d, v:v + 1], in1=hv,
                           op0=ALU.mult, op1=ALU.add)
                if (g0, g1) == FF_GROUPS[-1]:
                    self.cb_final(bi, ci)
        while self.mq is not None:
            self.mods_tick()

    def conv_mixer(self, l, bi, cis):
        j = l // 2
        self.norm(l, 1, bi, cis)
        self.fence()
        yb = self.salloc(2 * NTOK, BF16).rearrange("p (a b) -> p a b", a=2)
        Ry = {(a, ci): Res(f"y{a}_{ci}", True) for a in range(2) for ci in range(5)}
        bbuf = [self.salloc(NTOK, BF16) for _ in range(2)]
        Rb = [Res("bb0", True), Res("bb1", True)]
        self.alloc_nsq()
        VL = NTOK + 4
        vbuf = [self.salloc(VL, F32) for _ in range(2)]
        Rv = [Res("vb0", True), Res("vb1", True)]
        voff = lambda s: 1 if s == "x" else NX + 3
        for i in range(2):
            self.E("dve", "memset", [], [Rv[i]], vbuf[i][:, :], 0.0)
        wiv = self.sc_w_in[j].rearrange("(kc p) n -> p kc n", p=128)
        wov = self.sc_w_out[j].rearrange("(cc p) n -> p cc n", p=128)
        cvw = lambda tap, ch: self.gp[:, OFF_CV + (j * 3 + tap) * 8 + ch: OFF_CV + (j * 3 + tap) * 8 + ch + 1]
        it = 0
        for cp in range(4):
            ub, Rub = self.wget(wiv[:, :, cp * 256:(cp + 1) * 256], [128, 8, 256])
            uc, Ruc = self.wget(wiv[:, :, D + cp * 256:D + (cp + 1) * 256], [128, 8, 256])
            uu, Ruu = self.wget(wiv[:, :, 2 * D + cp * 256:2 * D + (cp + 1) * 256], [128, 8, 256])
            for cc in range(2):
                ch = cp * 2 + cc
                c0 = cc * 128
                bb, Rbb = bbuf[it % 2], Rb[it % 2]
                vb, Rvb = vbuf[it % 2], Rv[it % 2]
                it += 1
                for ci in cis:
                    s, t0, n, col = CHUNKS[ci]
                    pb, Rpb = self.psum()
                    pc, Rpc = self.psum()
                    pu, Rpu = self.psum()
                    for (pp, Rpp, uw, Ruw) in ((pb, Rpb, ub, Rub), (pc, Rpc, uc, Ruc), (pu, Rpu, uu, Ruu)):
                        for kc in range(8):
                            self.E("pe", "matmul", [Ruw, self.RnT[(ci, kc)]], [Rpp], pp[:, :n],
                                   lhsT=uw[:, kc, c0:c0 + 128], rhs=self.nT[:, kc, col:col + n],
                                   start=(kc == 0), stop=(kc == 7))
                    self.E("act", "activation", [Rpb], [Rbb], out=bb[:, col:col + n], in_=pb[:, :n], func=AF.Identity)
                    cs, Rcs = self.tf()
                    self.E("act", "activation", [Rpc], [Rcs], out=cs[:, :n], in_=pc[:, :n], func=AF.Identity)
                    vo = voff(s) + t0
                    self.E("dve", "tensor_tensor", [Rcs, Rpu], [Rvb], out=vb[:, vo:vo + n], in0=cs[:, :n],
                           in1=pu[:, :n], op=ALU.mult)
                for ci in cis:
                    s, t0, n, col = CHUNKS[ci]
                    vo = voff(s) + t0
                    a1, Ra1 = self.tf()
                    self.E("act", "activation", [Rvb, self.Rgp], [Ra1], out=a1[:, :n], in_=vb[:, vo:vo + n],
                           func=AF.Identity, scale=cvw(1, ch))
                    self.E("dve", "scalar_tensor_tensor", [Rvb, self.Rgp, Ra1], [Ra1], out=a1[:, :n],
                           in0=vb[:, vo - 1:vo - 1 + n], scalar=cvw(0, ch), in1=a1[:, :n], op0=ALU.mult, op1=ALU.add)
                    self.E("dve", "scalar_tensor_tensor", [Rvb, self.Rgp, Ra1], [Ra1], out=a1[:, :n],
                           in0=vb[:, vo + 1:vo + 1 + n], scalar=cvw(2, ch), in1=a1[:, :n], op0=ALU.mult, op1=ALU.add)
                    self.E("dve", "tensor_tensor", [Ra1, Rbb], [Ry[(cc, ci)]], out=yb[:, cc, col:col + n],
                           in0=a1[:, :n], in1=bb[:, col:col + n], op=ALU.mult)
            uo, Ruo = self.wget(wov[:, cp * 2:cp * 2 + 2, :], [128, 2, 1024])
            for ci in cis:
                s, t0, n, col = CHUNKS[ci]
                v = bi if s == "x" else 2
                for d in range(8):
                    po, Rpo = self.psum()
                    for cc in range(2):
                        self.E("pe", "matmul", [Ruo, Ry[(cc, ci)]], [Rpo], po[:, :n],
                               lhsT=uo[:, cc, d * 128:(d + 1) * 128], rhs=yb[:, cc, col:col + n],
                               start=(cc == 0), stop=(cc == 1))
                    hv = self.hview(ci, d)
                    self.E("dve", "scalar_tensor_tensor", [Rpo, self.RM[l], self.Rh[(ci, d)]], [self.Rh[(ci, d)]],
                           out=hv, in0=po[:, :n], scalar=self.M[:, l, 5 * 8 + d, v:v + 1], in1=hv,
                           op0=ALU.mult, op1=ALU.add)
                if cp == 3:
                    self.cb_final(bi, ci)

    def rstd_from(self, parts, n, dim):
        pss, Rpss = self.psum()
        for i, (pa, Rpa, P) in enumerate(parts):
            sq, Rsq = self.tb()
            self.E("act", "activation", [Rpa], [Rsq], out=sq[0:P, :n], in_=pa, func=AF.Square)
            self.E("pe", "matmul", [Rsq, self.Rones], [Rpss], pss[:, :n], lhsT=self.ones[0:P, :], rhs=sq[0:P, :n],
                   start=(i == 0), stop=(i == len(parts) - 1))
        rstd, Rrstd = self.tf()
        self.E("act", "activation", [Rpss, self.Rgp], [Rrstd], out=rstd[:, :n], in_=pss[:, :n], func=AF.Sqrt,
               scale=1.0 / dim, bias=self.gp[:, OFF_EPS:OFF_EPS + 1])
        self.E("dve", "reciprocal", [Rrstd], [Rrstd], out=rstd[:, :n], in_=rstd[:, :n])
        return rstd, Rrstd

    def mla_mixer(self, l, bi, ctx_q):
        j = l // 2
        cis = [0, 1, 2, 3, 4]
        self.norm(l, 1, bi, cis)
        self.fence()
        cqn = self.salloc(2 * NTOK, BF16).rearrange("p (a b) -> p a b", a=2)
        ckvn = self.salloc(NTOK, BF16)
        krp = self.salloc(NTOK, BF16)
        ssr = self.salloc(32, F32)
        sck2 = [self.salloc(32, F32) for _ in range(2)]
        sck_t = self.salloc(32, F32)
        sqk = self.salloc(512, BF16)
        Rsqk = Res("sqk", True)
        Rssr = Res("ssr", True)
        Rsck2 = [Res("sck0", True), Res("sck1", True)]
        Rsckt = Res("sckt", True)
        tabC = self.salloc(NX, BF16)
        tabS = self.salloc(NX, BF16)
        Kn2 = [self.salloc(NTOK, BF16) for _ in range(2)]
        Vh2 = [self.salloc(18 * 128, BF16).rearrange("p (a b) -> p a b", a=18) for _ in range(2)]
        qn = [self.salloc(512, BF16) for _ in range(2)]
        qr = [self.salloc(512, BF16) for _ in range(2)]
        Rcqn = {ci: Res(f"cqn{ci}", True) for ci in cis}
        Rckvn = {ci: Res(f"ckvn{ci}", True) for ci in cis}
        Rkrp = {ci: Res(f"krp{ci}", True) for ci in cis}
        Rtab = Res("tab", True)
        RKn2 = [{ci: Res(f"Kn{b}_{ci}", True) for ci in cis} for b in range(2)]
        RV2 = [{t: Res(f"V{b}_{t}", True) for t in range(18)} for b in range(2)]
        Rq = [Res("q0", True), Res("q1", True)]
        gcol = lambda off, c: self.gp[:, off + c: off + c + 1]

        wav = self.w_a[j].rearrange("(kc p) n -> p kc n", p=128)
        ua0, Rua0 = self.wget(wav[:, :, 0:256], [128, 8, 256])
        ua1, Rua1 = self.wget(wav[:, :, 256:512], [128, 8, 256])
        self.DMA("pool", tabC[0:64, :], self.rope_d[:, 0:NX], [], [Rtab])
        self.DMA("pool", tabS[0:64, :], self.rope_d[:, NX:2 * NX], [], [Rtab])
        for ci in cis:
            s, t0, n, col = CHUNKS[ci]
            outs = []
            for (uw, Ruw, c0, P) in ((ua0, Rua0, 0, 128), (ua0, Rua0, 128, 128), (ua1, Rua1, 0, 128),
                                     (ua1, Rua1, 128, 64), (ua1, Rua1, 192, 64)):
                pp, Rpp = self.psum()
                for kc in range(8):
                    self.E("pe", "matmul", [Ruw, self.RnT[(ci, kc)]], [Rpp], pp[0:P, :n],
                           lhsT=uw[:, kc, c0:c0 + P], rhs=self.nT[:, kc, col:col + n],
                           start=(kc == 0), stop=(kc == 7))
                outs.append((pp, Rpp))
            (pq0, Rpq0), (pq1, Rpq1), (pkv, Rpkv), (pkr, Rpkr), (pks, Rpks) = outs
            rs, Rrs = self.rstd_from([(pq0[:, :n], Rpq0, 128), (pq1[:, :n], Rpq1, 128)], n, 256)
            for a, (pq, Rpq) in enumerate(((pq0, Rpq0), (pq1, Rpq1))):
                self.E("dve", "scalar_tensor_tensor", [Rpq, self.Rgp, Rrs], [Rcqn[ci]], out=cqn[:, a, col:col + n],
                       in0=pq[:, :n], scalar=gcol(OFF_GQA, j * 2 + a), in1=rs[:, :n], op0=ALU.mult, op1=ALU.mult)
            rs2, Rrs2 = self.rstd_from([(pkv[:, :n], Rpkv, 128)], n, 128)
            self.E("dve", "scalar_tensor_tensor", [Rpkv, self.Rgp, Rrs2], [Rckvn[ci]], out=ckvn[:, col:col + n],
                   in0=pkv[:, :n], scalar=gcol(OFF_GKVA, j), in1=rs2[:, :n], op0=ALU.mult, op1=ALU.mult)
            ksq, Rksq = self.tb()
            self.E("act", "activation", [Rpkr], [Rksq], out=ksq[0:64, :n], in_=pkr[0:64, :n], func=AF.Square)
            for tt in range(n // 128):
                kt = col // 128 + tt
                self.E("pe", "matmul", [Rksq, self.Rones], [self.Rps[7]], self.ps[7][:, kt:kt + 1],
                       lhsT=ksq[0:64, tt * 128:(tt + 1) * 128], rhs=self.ones[0:64, 0:1], start=True, stop=True)
            if s == "x":
                t1, Rt1 = self.tf()
                t2, Rt2 = self.tf()
                self.E("dve", "scalar_tensor_tensor", [Rpkr, self.Rgp, Rtab], [Rt1], out=t1[0:64, :n],
                       in0=pkr[0:64, :n], scalar=self.gp[0:64, OFF_GK + j * 3 + 1:OFF_GK + j * 3 + 2],
                       in1=tabC[0:64, t0:t0 + n], op0=ALU.mult, op1=ALU.mult)
                self.E("dve", "scalar_tensor_tensor", [Rpks, self.Rgp, Rtab], [Rt2], out=t2[0:64, :n],
                       in0=pks[0:64, :n], scalar=self.gp[0:64, OFF_GK + j * 3 + 2:OFF_GK + j * 3 + 3],
                       in1=tabS[0:64, t0:t0 + n], op0=ALU.mult, op1=ALU.mult)
                self.E("dve", "tensor_tensor", [Rt1, Rt2], [Rkrp[ci]], out=krp[0:64, col:col + n], in0=t1[0:64, :n],
                       in1=t2[0:64, :n], op=ALU.add)
            else:
                self.E("act", "activation", [Rpkr, self.Rgp], [Rkrp[ci]], out=krp[0:64, col:col + n],
                       in_=pkr[0:64, :n], func=AF.Identity,
                       scale=self.gp[0:64, OFF_GK + j * 3 + 1:OFF_GK + j * 3 + 2])

        self.E("dve", "tensor_copy", [self.Rps[7]], [Rssr], out=ssr[:, 0:18], in_=self.ps[7][:, 0:18])
        Ro = {(h, ci): self.RnT[(ci, h)] for h in range(H) for ci in cis}
        ukv, Rukv = self.wget(self.w_ukv[j], [128, 2048])
        uqn = [self.wget(self.w_uqn[j].rearrange("(kc p) n -> p kc n", p=128)[:, :, hh * 512:(hh + 1) * 512],
                         [128, 2, 512]) for hh in range(2)]
        uqr = [self.wget(self.w_uqr[j].rearrange("(kc p) n -> p kc n", p=128)[:, :, hh * 512:(hh + 1) * 512],
                         [128, 2, 512]) for hh in range(2)]
        SC = 192.0 ** -0.5
        qcis = cis if ctx_q else [0, 1, 2, 3]
        nq = len(qcis)
        sqn_b = self.salloc(512, BF16)
        sqr_b = self.salloc(512, BF16)
        Rsqq = Res("sqq", True)
        epsap = self.gp[:, OFF_EPS:OFF_EPS + 1]
        LOOK = 3
        psB = [3, 4]
        st = {"b": 0, "q": 0}
        qst = {}

        def psumB():
            i = psB[st["b"] % len(psB)]
            st["b"] += 1
            return self.ps[i], self.Rps[i]

        def k_s1(h, ci):
            b = h % 2
            s_, t0, n, col = CHUNKS[ci]
            pk, Rpk = psumB()
            self.E("pe", "matmul", [Rukv, Rckvn[ci]], [Rpk], pk[:, :n], lhsT=ukv[:, h * 256:h * 256 + 128],
                   rhs=ckvn[:, col:col + n], start=True, stop=True)
            self.E("act", "activation", [Rpk, self.Rgp], [RKn2[b][ci]], out=Kn2[b][:, col:col + n], in_=pk[:, :n],
                   func=AF.Identity, scale=gcol(OFF_GK, j * 3))
            self.E("act", "activation", [Rpk], [Rsqk], out=sqk[:, :n], in_=pk[:, :n], func=AF.Square)
            pv, Rpv = psumB()
            nt = n // 128
            for tt in range(nt):
                self.E("pe", "matmul", [Rukv, Rckvn[ci]], [Rpv], pv[:, tt * 128:(tt + 1) * 128],
                       lhsT=ckvn[:, col + tt * 128: col + (tt + 1) * 128],
                       rhs=ukv[:, h * 256 + 128:h * 256 + 256], start=True, stop=True)
            tk0 = col // 128
            self.E("act", "activation", [Rpv], [RV2[b][tk0 + tt] for tt in range(nt)],
                   out=Vh2[b][:, tk0:tk0 + nt, :], in_=pv[:, :n].rearrange("p (a b) -> p a b", a=nt),
                   func=AF.Identity)

        def k_s2(h, ci):
            b = h % 2
            s_, t0, n, col = CHUNKS[ci]
            tk0 = col // 128
            for tt in range(n // 128):
                c = b * 32 + tk0 + tt
                self.E("pe", "matmul", [Rsqk, self.Rones], [self.Rps[5]], self.ps[5][:, c:c + 1],
                       lhsT=sqk[:, tt * 128:(tt + 1) * 128], rhs=self.ones[:, 0:1], start=True, stop=True)

        def k_fin(h):
            b = h % 2
            self.E("dve", "tensor_tensor", [self.Rps[5], Rssr], [Rsckt], out=sck_t[:, 0:18],
                   in0=self.ps[5][:, b * 32:b * 32 + 18], in1=ssr[:, 0:18], op=ALU.add)
            self.E("act", "activation", [Rsckt, self.Rgp], [Rsckt], out=sck_t[:, 0:18], in_=sck_t[:, 0:18],
                   func=AF.Sqrt, scale=1.0 / 192, bias=epsap)
            self.E("dve", "reciprocal", [Rsckt], [Rsckt], out=sck_t[:, 0:18], in_=sck_t[:, 0:18])
            self.E("dve", "tensor_scalar", [Rsckt], [Rsck2[b]], out=sck2[b][:, 0:18], in0=sck_t[:, 0:18], scalar1=SC,
                   scalar2=None, op0=ALU.mult)

        def q_sA(h, qi):
            ci = qcis[qi]
            s_, t0, n, col = CHUNKS[ci]
            uqn_h, Ruqn_h = uqn[h // 4]
            uqr_h, Ruqr_h = uqr[h // 4]
            hh = h % 4
            pqn, Rpqn = self.ps[0], self.Rps[0]
            pqr, Rpqr = self.ps[1], self.Rps[1]
            pqs, Rpqs = self.ps[2], self.Rps[2]
            for kc in range(2):
                self.E("pe", "matmul", [Ruqn_h, Rcqn[ci]], [Rpqn], pqn[:, :n],
                       lhsT=uqn_h[:, kc, hh * 128:(hh + 1) * 128], rhs=cqn[:, kc, col:col + n],
                       start=(kc == 0), stop=(kc == 1))
            for kc in range(2):
                self.E("pe", "matmul", [Ruqr_h, Rcqn[ci]], [Rpqr], pqr[0:64, :n],
                       lhsT=uqr_h[:, kc, hh * 128:hh * 128 + 64], rhs=cqn[:, kc, col:col + n],
                       start=(kc == 0), stop=(kc == 1))
            for kc in range(2):
                self.E("pe", "matmul", [Ruqr_h, Rcqn[ci]], [Rpqs], pqs[0:64, :n],
                       lhsT=uqr_h[:, kc, hh * 128 + 64:hh * 128 + 128], rhs=cqn[:, kc, col:col + n],
                       start=(kc == 0), stop=(kc == 1))
            self.E("act", "activation", [Rpqn], [Rsqq], out=sqn_b[:, :n], in_=pqn[:, :n], func=AF.Square)
            self.E("act", "activation", [Rpqr], [Rsqq], out=sqr_b[0:64, :n], in_=pqr[0:64, :n], func=AF.Square)

        def q_sB(h, qi):
            ci = qcis[qi]
            s_, t0, n, col = CHUNKS[ci]
            pqn, Rpqn = self.ps[0], self.Rps[0]
            pqr, Rpqr = self.ps[1], self.Rps[1]
            pqs, Rpqs = self.ps[2], self.Rps[2]
            qb = st["q"] % 2
            st["q"] += 1
            qst[(h, qi)] = qb
            pss, Rpss = psumB()
            self.E("pe", "matmul", [Rsqq, self.Rones], [Rpss], pss[:, :n], lhsT=self.ones[:], rhs=sqn_b[:, :n],
                   start=True, stop=False)
            self.E("pe", "matmul", [Rsqq, self.Rones], [Rpss], pss[:, :n], lhsT=self.ones[0:64, :],
                   rhs=sqr_b[0:64, :n], start=False, stop=True)
            rs, Rrs = self.tf()
            self.E("act", "activation", [Rpss, self.Rgp], [Rrs], out=rs[:, :n], in_=pss[:, :n], func=AF.Sqrt,
                   scale=1.0 / 192, bias=epsap)
            self.E("dve", "reciprocal", [Rrs], [Rrs], out=rs[:, :n], in_=rs[:, :n])
            self.E("dve", "scalar_tensor_tensor", [Rpqn, self.Rgp, Rrs], [Rq[qb]], out=qn[qb][:, :n],
                   in0=pqn[:, :n], scalar=gcol(OFF_GQ, j * 3), in1=rs[:, :n], op0=ALU.mult, op1=ALU.mult)
            if s_ == "x":
                t1, Rt1 = self.tf()
                t2, Rt2 = self.tf()
                self.E("dve", "scalar_tensor_tensor", [Rpqr, self.Rgp, Rtab], [Rt1], out=t1[0:64, :n],
                       in0=pqr[0:64, :n], scalar=self.gp[0:64, OFF_GQ + j * 3 + 1:OFF_GQ + j * 3 + 2],
                       in1=tabC[0:64, t0:t0 + n], op0=ALU.mult, op1=ALU.mult)
                self.E("dve", "scalar_tensor_tensor", [Rpqs, self.Rgp, Rtab], [Rt2], out=t2[0:64, :n],
                       in0=pqs[0:64, :n], scalar=self.gp[0:64, OFF_GQ + j * 3 + 2:OFF_GQ + j * 3 + 3],
                       in1=tabS[0:64, t0:t0 + n], op0=ALU.mult, op1=ALU.mult)
                self.E("dve", "tensor_tensor", [Rt1, Rt2], [Rt1], out=t1[0:64, :n], in0=t1[0:64, :n],
                       in1=t2[0:64, :n], op=ALU.add)
                self.E("dve", "tensor_tensor", [Rt1, Rrs], [Rq[qb]], out=qr[qb][0:64, :n], in0=t1[0:64, :n],
                       in1=rs[0:64, :n], op=ALU.mult)
            else:
                self.E("dve", "scalar_tensor_tensor", [Rpqr, self.Rgp, Rrs], [Rq[qb]], out=qr[qb][0:64, :n],
                       in0=pqr[0:64, :n], scalar=self.gp[0:64, OFF_GQ + j * 3 + 1:OFF_GQ + j * 3 + 2],
                       in1=rs[0:64, :n], op0=ALU.mult, op1=ALU.mult)

        def attention(h, qi, hooks):
            b = h % 2
            ci = qcis[qi]
            s_, t0, n, col = CHUNKS[ci]
            qb = qst[(h, qi)]
            kts = list(range(18)) if s_ == "x" else [16, 17]
            nk = len(kts)
            po, Rpo = self.ps[6], self.Rps[6]
            pd, Rpd = self.ps[7], self.Rps[7]
            pts = {}
            hooks = sorted(hooks, key=lambda x: x[0])
            hi = 0
            for step in range(nk + LOOK):
                if step < nk:
                    kt = kts[step]
                    kci = min(kt // 4, 4)
                    psc, Rpsc = psumB()
                    self.E("pe", "matmul", [RKn2[b][kci], Rq[qb]], [Rpsc], psc[:, :n],
                           lhsT=Kn2[b][:, kt * 128:(kt + 1) * 128], rhs=qn[qb][:, :n], start=True, stop=False)
                    self.E("pe", "matmul", [Rkrp[kci], Rq[qb]], [Rpsc], psc[:, :n],
                           lhsT=krp[0:64, kt * 128:(kt + 1) * 128], rhs=qr[qb][0:64, :n], start=False, stop=True)
                    pt, Rpt = self.tb()
                    self.E("act", "activation", [Rpsc, Rsck2[b]], [Rpt], out=pt[:, :n], in_=psc[:, :n], func=AF.Exp,
                           scale=sck2[b][:, kt:kt + 1])
                    pts[step] = (pt, Rpt)
                while hi < len(hooks) and hooks[hi][0] <= step:
                    hooks[hi][1]()
                    hi += 1
                if step >= LOOK:
                    ki = step - LOOK
                    kt = kts[ki]
                    pt, Rpt = pts.pop(ki)
                    self.E("pe", "matmul", [RV2[b][kt], Rpt], [Rpo], po[:, :n], lhsT=Vh2[b][:, kt, :], rhs=pt[:, :n],
                           start=(ki == 0), stop=(ki == nk - 1))
                    self.E("pe", "matmul", [self.Rones, Rpt], [Rpd], pd[:, :n], lhsT=self.ones[:], rhs=pt[:, :n],
                           start=(ki == 0), stop=(ki == nk - 1))
            while hi < len(hooks):
                hooks[hi][1]()
                hi += 1
            rd, Rrd = self.tf()
            self.E("dve", "reciprocal", [Rpd], [Rrd], out=rd[:, :n], in_=pd[:, :n])
            self.E("dve", "tensor_tensor", [Rpo, Rrd], [Ro[(h, ci)]], out=self.nT[:, h, col:col + n],
                   in0=po[:, :n], in1=rd[:, :n], op=ALU.mult)

        for ci in cis:
            k_s1(0, ci)
            k_s2(0, ci)
        k_fin(0)
        q_sA(0, 0)
        q_sB(0, 0)
        items = [(h, qi) for h in range(H) for qi in range(nq)]
        kassign = {0: [0], 1: [1], 2: [2], 3: [3, 4], 4: []}
        for idx, (h, qi) in enumerate(items):
            hooks = []
            if idx + 1 < len(items):
                h2, q2 = items[idx + 1]
                hooks.append((1, (lambda a=h2, c=q2: q_sA(a, c))))
                hooks.append((6, (lambda a=h2, c=q2: q_sB(a, c))))
            if h + 1 < H:
                stp = 3
                for kc_ in kassign[qi]:
                    hooks.append((stp, (lambda a=h + 1, c=kc_: k_s1(a, c))))
                    hooks.append((stp + 5, (lambda a=h + 1, c=kc_: k_s2(a, c))))
                    stp += 8
                if qi == 3:
                    hooks.append((19, (lambda a=h + 1: k_fin(a))))
            attention(h, qi, hooks)

        wov = self.w_o[j].rearrange("(hh p) n -> p hh n", p=128)
        uos = [self.wget(wov[:, :, dp * 256:(dp + 1) * 256], [128, 8, 256]) for dp in range(4)]
        self.fence()
        self.alloc_nsq()
        for ci in qcis:
            s, t0, n, col = CHUNKS[ci]
            v = bi if s == "x" else 2
            for d in range(8):
                uo, Ruo = uos[d // 2]
                dd = d % 2
                po, Rpo = self.psum()
                for h in range(H):
                    self.E("pe", "matmul", [Ruo, Ro[(h, ci)]], [Rpo], po[:, :n],
                           lhsT=uo[:, h, dd * 128:(dd + 1) * 128], rhs=self.nT[:, h, col:col + n],
                           start=(h == 0), stop=(h == H - 1))
                hv = self.hview(ci, d)
                self.E("dve", "scalar_tensor_tensor", [Rpo, self.RM[l], self.Rh[(ci, d)]], [self.Rh[(ci, d)]],
                       out=hv, in0=po[:, :n], scalar=self.M[:, l, 5 * 8 + d, v:v + 1], in1=hv,
                       op0=ALU.mult, op1=ALU.add)
            self.cb_final(bi, ci)

    def run_batch(self, bi):
        parts = os.environ.get("KTEST_PARTS", "f1,mix,f2").split(",")
        overlap = os.environ.get("KTEST_NOOVERLAP", "") == ""
        phases = []
        xs = [0, 1, 2, 3]
        for l in range(self.n_layers):
            kind = l % 2
            last = (l == DEPTH - 1)
            rin = (not last) or kind == 1
            rout = not last
            if "f1" in parts:
                phases.append(("ffn", l, 0, xs + ([4] if rin else []), 0))
            if "mix" in parts:
                if kind == 0:
                    phases.append(("conv", l, None, xs + ([4] if rout else []), 1))
                else:
                    phases.append(("mla", l, rout, [0, 1, 2, 3, 4], 1))
            if "f2" in parts:
                phases.append(("ffn", l, 1, xs + ([4] if rout else []), 2))
        self.pre = set()
        for i, (kind, l, arg, cis, k) in enumerate(phases):
            self.nxt = None
            if overlap and i + 1 < len(phases):
                n_ = phases[i + 1]
                self.nxt = (n_[1], n_[4], set(n_[3]))
            self.pre_next = set()
            self.pending = None
            if kind == "ffn":
                self.ffn(l, arg, bi, cis)
            elif kind == "conv":
                self.conv_mixer(l, bi, cis)
            else:
                self.mla_mixer(l, bi, arg)
            self.cb_flush(bi)
            self.pre = self.pre_next


def _rope_tables():
    t = np.arange(NX)
    r = (t // 64).astype(np.float32)
    c = (t % 64).astype(np.float32)
    inv = (10000.0 ** (-np.arange(16, dtype=np.float32) / 16)).astype(np.float32)
    ang = [r[None, :] * inv[:, None], c[None, :] * inv[:, None]]
    C = np.zeros((64, NX), np.float32)
    S = np.zeros((64, NX), np.float32)
    for a in range(2):
        for half in range(2):
            rows = slice(a * 32 + half * 16, a * 32 + half * 16 + 16)
            C[rows] = np.cos(ang[a])
            S[rows] = np.sin(ang[a]) * (-1.0 if half == 0 else 1.0)
    return np.concatenate([C, S], axis=1).astype(np.float32)


def _swap_idx():
    d = np.arange(64)
    a, half, f = d // 32, (d % 32) // 16, d % 16
    return a * 32 + (1 - half) * 16 + f


_CACHE = {}


def kernel(x, c, ctx, c_ctx, w_mod, b_mod, g_norm, ffn_w1, ffn_w3, ffn_w2, sc_w_in, sc_conv, sc_w_out,
           mla_w_a, mla_g_qa, mla_w_uq, mla_g_kva, mla_w_ukv, mla_g_q, mla_g_k, mla_w_o):
    n_layers = int(os.environ.get("KTEST_LAYERS", DEPTH))
    n_cores = int(os.environ.get("KTEST_CORES", 8))
    nb = 2
    f = lambda a: np.ascontiguousarray(np.asarray(a, dtype=np.float32))
    x, c, ctx, c_ctx = f(x), f(c), f(ctx), f(c_ctx)
    sw = _swap_idx()
    gp = np.zeros((128, NG), np.float32)
    gn = f(g_norm)
    for l in range(DEPTH):
        for k in range(3):
            gp[:, OFF_GN + (l * 3 + k) * 8: OFF_GN + (l * 3 + k + 1) * 8] = gn[l, k].reshape(8, 128).T
        gp[:, OFF_BM + l * 72: OFF_BM + (l + 1) * 72] = f(b_mod)[l].reshape(72, 128).T
    for j in range(2):
        for tap in range(3):
            gp[:, OFF_CV + (j * 3 + tap) * 8: OFF_CV + (j * 3 + tap + 1) * 8] = f(sc_conv)[j, tap].reshape(8, 128).T
        gp[:, OFF_GQA + j * 2: OFF_GQA + j * 2 + 2] = f(mla_g_qa)[j].reshape(2, 128).T
        gp[:, OFF_GKVA + j] = f(mla_g_kva)[j]
        for off, g in ((OFF_GQ, f(mla_g_q)[j]), (OFF_GK, f(mla_g_k)[j])):
            gp[:, off + j * 3] = g[:128]
            gp[:64, off + j * 3 + 1] = g[128:]
            gp[:64, off + j * 3 + 2] = g[128:][sw]
    gp[:, OFF_EPS] = EPS
    wa = f(mla_w_a)
    w_a = np.ascontiguousarray(np.concatenate([wa, wa[:, :, 384:448][:, :, sw]], axis=2))
    wuq = f(mla_w_uq).reshape(2, 256, H, 192)
    w_uqn = np.ascontiguousarray(wuq[:, :, :, :128].reshape(2, 256, H * 128))
    w_uqr = np.ascontiguousarray(np.concatenate([wuq[:, :, :, 128:], wuq[:, :, :, 128:][:, :, :, sw]], axis=3).reshape(2, 256, H * 128))
    rope = _rope_tables()
    shared = {
        "gpack": gp, "rope": rope, "w_mod": f(w_mod), "ffn_w1": f(ffn_w1), "ffn_w3": f(ffn_w3), "ffn_w2": f(ffn_w2),
        "sc_w_in": f(sc_w_in), "sc_w_out": f(sc_w_out), "w_a": w_a, "w_uqn": w_uqn, "w_uqr": w_uqr,
        "w_ukv": f(mla_w_ukv), "w_o": f(mla_w_o),
    }
    in_maps = []
    for i in range(n_cores):
        bs = slice(i * nb, (i + 1) * nb)
        m = dict(shared)
        m["xT"] = np.ascontiguousarray(x[bs].transpose(0, 2, 1))
        m["cT"] = np.ascontiguousarray(ctx[bs].transpose(0, 2, 1))
        m["condT"] = np.ascontiguousarray(np.stack([c[i * nb], c[i * nb + 1], c_ctx], axis=1))
        in_maps.append(m)
    key = (n_layers, nb)
    if key not in _CACHE:
        _CACHE[key] = Builder(n_layers, nb).build()
    nc = _CACHE[key]
    res = run_bass_kernel_spmd(nc, in_maps, core_ids=list(range(n_cores)))
    out = np.concatenate([r["yT"].transpose(0, 2, 1) for r in res.results], axis=0)
    if os.environ.get("KTEST_DBG", ""):
        _CACHE["dbg"] = res.results[0]
    return np.ascontiguousarray(out.astype(np.float32))
```

```python
import os
import math
from contextlib import ExitStack
import numpy as np
import concourse.bass as bass
import concourse.mybir as mybir
from concourse.bass_utils import run_bass_kernel_spmd

F32 = mybir.dt.float32
BF16 = mybir.dt.bfloat16
ALU = mybir.AluOpType
AF = mybir.ActivationFunctionType

ENGS = ["pe", "act", "dve", "pool", "sp"]
NDMA_SEM = 12


class Res:
    __slots__ = ("name", "w", "r", "rd", "scr")

    def __init__(self, name="", scr=False):
        self.name = name
        self.w = None
        self.r = {}
        self.rd = []
        self.scr = scr


class Op:
    __slots__ = ("eng", "fn", "deps", "key", "val", "signalled", "is_dma", "waits", "snap")


class Sched:
    def __init__(self):
        self.ops = []
        self.per_eng = {e: [] for e in ENGS}
        self.ndma = {"sp": 0, "pool": 0, "act": 0}
        self.dma_ops = {"sp": [], "pool": [], "act": []}

    def emit(self, eng, fn, reads=(), writes=(), dma=False):
        op = Op()
        op.eng = eng
        op.fn = fn
        op.is_dma = dma
        op.signalled = dma
        deps = set()
        for r in reads:
            if r.w is not None:
                deps.add(r.w)
        for w in writes:
            if w.w is not None:
                deps.add(w.w)
            deps.update(w.r.values())
            deps.update(w.rd)
        if dma:
            i = self.ndma[eng]
            self.ndma[eng] += 1
            op.key = ("dma", eng, i % NDMA_SEM)
            op.val = 16 * (i // NDMA_SEM + 1)
            if i >= NDMA_SEM:
                deps.add(self.dma_ops[eng][i - NDMA_SEM])
            self.dma_ops[eng].append(op)
        else:
            op.key = eng
            op.val = None
        if eng == "pe" and not dma:
            deps = {d for d in deps if d.is_dma or d.eng != "pe"}
        op.deps = deps
        for d in deps:
            d.signalled = True
        for r in reads:
            if dma:
                r.rd.append(op)
            else:
                r.r[eng] = op
        for w in writes:
            w.w = op
            w.r = {}
            w.rd = []
        self.ops.append(op)
        self.per_eng[eng].append(op)
        return op

    def finalize(self):
        counts = {e: 0 for e in ENGS}
        known = {e: {} for e in ENGS}
        for op in self.ops:
            kn = known[op.eng]
            waits = {}
            for d in op.deps:
                if kn.get(d.key, 0) >= d.val:
                    continue
                if waits.get(d.key, 0) < d.val:
                    waits[d.key] = d.val
            for d in op.deps:
                for k, v in d.snap.items():
                    if kn.get(k, 0) < v:
                        kn[k] = v
            op.waits = waits
            if not op.is_dma and op.signalled:
                counts[op.eng] += 1
                op.val = counts[op.eng]
            if op.signalled:
                snap = dict(kn)
                if snap.get(op.key, 0) < op.val:
                    snap[op.key] = op.val
                op.snap = snap
            else:
                op.snap = None
            op.deps = None
        self.counts = counts

    def run(self, block, sems):
        per_eng = self.per_eng

        def replay(ename, eng):
            for op in per_eng[ename]:
                for k, v in op.waits.items():
                    eng.wait_ge(sems[k], v)
                ins = op.fn(eng)
                if op.signalled:
                    ins.then_inc(sems[op.key], 16 if op.is_dma else 1)

        @block.tensor
        def _(eng):
            replay("pe", eng)

        @block.scalar
        def _(eng):
            replay("act", eng)

        @block.vector
        def _(eng):
            replay("dve", eng)

        @block.gpsimd
        def _(eng):
            replay("pool", eng)
            last = {}
            for op in self.dma_ops["pool"]:
                last[op.key] = op.val
            for k, v in last.items():
                eng.wait_ge(sems[k], v)

        @block.sync
        def _(eng):
            replay("sp", eng)
            last = {}
            for op in self.dma_ops["sp"]:
                last[op.key] = op.val
            for k, v in last.items():
                eng.wait_ge(sems[k], v)


D = 1024
NX = 2048
NCX = 256
NTOK = NX + NCX
DFF = 2816
NFF = DFF // 128
DEPTH = 4
H = 8
EPS = 1e-6
CHUNKS = [("x", 0, 512, 0), ("x", 512, 512, 512), ("x", 1024, 512, 1024), ("x", 1536, 512, 1536),
          ("c", 0, 256, 2048)]
FF_GROUPS = [(0, 6), (6, 14), (14, 22)]

OFF_GN, OFF_BM, OFF_CV, OFF_GQA, OFF_GKVA, OFF_GQ, OFF_GK, OFF_EPS, NG = 0, 96, 384, 432, 436, 438, 444, 450, 452
WSLOT = 2048
NSLOT = 6
SCR_BYTES = 53248


class Builder:
    def __init__(self, n_layers=DEPTH, nb=2):
        self.n_layers = n_layers
        self.nb = nb
        self.S = Sched()
        self.nc = bass.Bass("TRN2", target_bir_lowering=False)

    def E(self, eng, method, reads, writes, *args, **kw):
        if any(r.scr for r in reads) or any(w.scr for w in writes):
            reads = list(reads) + [self.RE]
        return self.S.emit(eng, lambda e: getattr(e, method)(*args, **kw), reads, writes)

    def DMA(self, q, out, in_, reads, writes):
        if any(r.scr for r in reads) or any(w.scr for w in writes):
            reads = list(reads) + [self.RE]
        return self.S.emit(q, lambda e: e.dma_start(out=out, in_=in_), reads, writes, dma=True)

    def psum(self):
        i = self.ps_i
        self.ps_i = (i + 1) % 6
        return self.ps[i], self.Rps[i]

    def tf(self):
        i = self.tf_i
        self.tf_i = (i + 1) % len(self.TF)
        return self.TF[i], self.RTF[i]

    def tb(self):
        i = self.tb_i
        self.tb_i = (i + 1) % len(self.TB)
        return self.TB[i], self.RTB[i]

    def fence(self):
        self.E("dve", "memset", [], [self.RE, self.Rdummy], self.dummy[:, 0:1], 0.0)
        self.scr_off = 0

    def salloc(self, nelem, dt):
        bpe = 2 if dt == BF16 else 4
        off = (self.scr_off + 3) // 4 * 4
        nbytes = nelem * bpe
        assert off + nbytes <= SCR_BYTES, (off, nbytes)
        self.scr_off = off + nbytes
        v = self.scr[:, off // 2: (off + nbytes) // 2]
        if dt != BF16:
            v = v.bitcast(dt)
        return v

    def wget(self, src, shape):
        i = self.w_i
        self.w_i += 1
        s = i % NSLOT
        n = int(np.prod(shape[1:]))
        assert n <= WSLOT
        v = self.wring[s][:, 0:n]
        if len(shape) == 3:
            v = v.rearrange("p (a b) -> p a b", a=shape[1])
        self.DMA("pool", v, src, [], [self.Rw[s]])
        return v, self.Rw[s]

    def build(self):
        nc = self.nc
        nb = self.nb
        dr = lambda name, shape: nc.dram_tensor(name, shape, F32, kind="ExternalInput").ap()
        self.xT = dr("xT", [nb, D, NX])
        self.cT = dr("cT", [nb, D, NCX])
        self.condT = dr("condT", [D, 3])
        self.gpack_d = dr("gpack", [128, NG])
        self.rope_d = dr("rope", [64, 2 * NX])
        self.w_mod = dr("w_mod", [DEPTH, D, 9 * D])
        self.w1 = dr("ffn_w1", [DEPTH, 2, D, DFF])
        self.w3 = dr("ffn_w3", [DEPTH, 2, D, DFF])
        self.w2 = dr("ffn_w2", [DEPTH, 2, DFF, D])
        self.sc_w_in = dr("sc_w_in", [2, D, 3 * D])
        self.sc_w_out = dr("sc_w_out", [2, D, D])
        self.w_a = dr("w_a", [2, D, 512])
        self.w_uqn = dr("w_uqn", [2, 256, H * 128])
        self.w_uqr = dr("w_uqr", [2, 256, H * 128])
        self.w_ukv = dr("w_ukv", [2, 128, 2048])
        self.w_o = dr("w_o", [2, D, D])
        self.yT = nc.dram_tensor("yT", [nb, D, NX], F32, kind="ExternalOutput").ap()
        self.dbg = os.environ.get("KTEST_DBG", "")
        if self.dbg:
            self.dbgM = nc.dram_tensor("dbgM", [128, DEPTH * 72 * 3], F32, kind="ExternalOutput").ap()
            self.dbgN = nc.dram_tensor("dbgN", [128, 8 * NTOK], F32, kind="ExternalOutput").ap()

        with ExitStack() as es:
            sb = lambda name, shape, dt: es.enter_context(nc.sbuf_tensor(name, shape, dt))
            self.hx = sb("hx", [128, 8, NX], F32)
            self.hc = sb("hc", [128, 8, NCX], F32)
            self.nT = sb("nT", [128, 8, NTOK], BF16)
            self.wring = [sb(f"wr{i}", [128, WSLOT], BF16) for i in range(NSLOT)]
            self.M = sb("M", [128, DEPTH, 72, 3], F32)
            self.DRV = sb("DRV", [128, DEPTH, 5, 8, 3], F32)
            self.gp = sb("gp", [128, NG], F32)
            self.ones = sb("ones_t", [128, 128], BF16)
            self.TF = [sb(f"tf{i}", [128, 512], F32) for i in range(4)]
            self.TB = [sb(f"tb{i}", [128, 512], BF16) for i in range(4)]
            self.TR = [sb(f"tr{i}", [128, 512], F32) for i in range(2)]
            self.RTR = [Res(f"tr{i}") for i in range(2)]
            self.tr_i = 0
            self.cond = sb("cond_t", [128, 8, 3], F32)
            self.scond = sb("scond", [128, 8, 3], BF16)
            self.dummy = sb("dummy_t", [128, 4], F32)
            self.scr = sb("scr", [128, SCR_BYTES // 2], BF16)
            self.ps = [es.enter_context(nc.psum_tensor(f"ps{i}", [128, 512], F32)) for i in range(8)]
            sems = {e: es.enter_context(nc.semaphore("s_" + e)) for e in ENGS}
            for q in ("sp", "pool"):
                for j in range(NDMA_SEM):
                    sems[("dma", q, j)] = es.enter_context(nc.semaphore(f"d_{q}{j}"))
            block = es.enter_context(nc.Block())

            self.Rh = {}
            for ci, (s, t0, n, col) in enumerate(CHUNKS):
                for kc in range(8):
                    self.Rh[(ci, kc)] = Res(f"h{ci}_{kc}")
            self.RnT = {(ci, kc): Res(f"n{ci}_{kc}") for ci in range(5) for kc in range(8)}
            self.Rw = [Res(f"w{i}") for i in range(NSLOT)]
            self.Rps = [Res(f"ps{i}") for i in range(8)]
            self.RTF = [Res(f"tf{i}") for i in range(4)]
            self.RTB = [Res(f"tb{i}") for i in range(4)]
            self.RM = [Res(f"M{l}") for l in range(DEPTH)]
            self.RDRV = [Res(f"DRV{l}") for l in range(DEPTH)]
            self.Rgp = Res("gp")
            self.Rones = Res("ones")
            self.Rcond = Res("cond")
            self.Rscond = Res("scond")
            self.RE = Res("epoch")
            self.Rdummy = Res("dummy")
            self.ps_i = self.tf_i = self.tb_i = self.w_i = 0
            self.mq = None
            self.scr_off = 0

            self.prologue()
            for bi in range(nb):
                self.load_h(bi)
                self.run_batch(bi)
                self.store_h(bi)

            self.S.finalize()
            self.S.run(block, sems)
        return nc

    def hview(self, ci, kc):
        s, t0, n, col = CHUNKS[ci]
        return (self.hx if s == "x" else self.hc)[:, kc, t0:t0 + n]

    def prologue(self):
        self.DMA("sp", self.gp[:], self.gpack_d, [], [self.Rgp])
        self.DMA("sp", self.cond[:], self.condT.rearrange("(kc p) v -> p kc v", p=128), [], [self.Rcond])
        self.E("dve", "memset", [], [self.Rones], self.ones[:], 1.0)
        self.E("act", "activation", [self.Rcond], [self.Rscond], out=self.scond[:], in_=self.cond[:], func=AF.Silu)
        self.fence()
        NST = 3
        stage = [self.salloc(8 * 512, BF16).rearrange("p (a b) -> p a b", a=8) for _ in range(NST)]
        Rst = [Res(f"st{i}", True) for i in range(NST)]
        si = 0
        for l in range(1):
            pM, RpM = self.psum()
            wv = self.w_mod[l].rearrange("(kc p) n -> p kc n", p=128)
            for n0 in range(0, 9 * D, 512):
                st, Rs = stage[si % NST], Rst[si % NST]
                si += 1
                self.DMA("pool", st, wv[:, :, n0:n0 + 512], [], [Rs])
                for oc in range(4):
                    j = n0 // 128 + oc
                    for kc in range(8):
                        self.E("pe", "matmul", [Rs, self.Rscond], [RpM], pM[:, 3 * j:3 * j + 3],
                               lhsT=st[:, kc, oc * 128:(oc + 1) * 128], rhs=self.scond[:, kc, :],
                               start=(kc == 0), stop=(kc == 7))
            self.mods_finish(l, pM, RpM)

    def mods_finish(self, l, pM, RpM):
        bm = self.gp[:, OFF_BM + l * 72: OFF_BM + (l + 1) * 72].unsqueeze(2).to_broadcast([128, 72, 3])
        self.E("dve", "tensor_tensor", [RpM, self.Rgp], [self.RM[l]], out=self.M[:, l, :, :],
               in0=pM[:, 0:216].rearrange("p (j v) -> p j v", v=3), in1=bm, op=ALU.add)
        for k in range(3):
            sc_ = self.M[:, l, (3 * k + 1) * 8:(3 * k + 2) * 8, :]
            g = self.gp[:, OFF_GN + (l * 3 + k) * 8: OFF_GN + (l * 3 + k + 1) * 8].unsqueeze(2).to_broadcast([128, 8, 3])
            self.E("dve", "scalar_tensor_tensor", [self.RM[l], self.Rgp], [self.RDRV[l]],
                   out=self.DRV[:, l, k, :, :], in0=sc_, scalar=1.0, in1=g, op0=ALU.add, op1=ALU.mult)
        for k, which in ((3, 2), (4, 8)):
            self.E("dve", "tensor_scalar", [self.RM[l]], [self.RDRV[l]], out=self.DRV[:, l, k, :, :],
                   in0=self.M[:, l, which * 8:(which + 1) * 8, :], scalar1=0.5, scalar2=None, op0=ALU.mult)

    def mods_begin(self, l):
        self.mq = {"l": l, "issued": 0, "done": 0,
                   "st": [(self.salloc(8 * 128, BF16).rearrange("p (a b) -> p a b", a=8), Res(f"mst{i}", True))
                          for i in range(3)]}

    def mods_tick(self):
        mq = self.mq
        if mq is None:
            return
        l = mq["l"]
        wv = self.w_mod[l].rearrange("(kc p) n -> p kc n", p=128)
        while mq["issued"] < 72 and mq["issued"] < mq["done"] + 3:
            jj = mq["issued"]
            st, Rs = mq["st"][jj % 3]
            self.DMA("pool", st, wv[:, :, jj * 128:(jj + 1) * 128], [], [Rs])
            mq["issued"] += 1
        jj = mq["done"]
        st, Rs = mq["st"][jj % 3]
        pM, RpM = self.ps[6], self.Rps[6]
        for kc in range(8):
            self.E("pe", "matmul", [Rs, self.Rscond], [RpM], pM[:, 3 * jj:3 * jj + 3], lhsT=st[:, kc, :],
                   rhs=self.scond[:, kc, :], start=(kc == 0), stop=(kc == 7))
        mq["done"] += 1
        if mq["done"] == 72:
            self.mods_finish(l, pM, RpM)
            self.mq = None

    def load_h(self, bi):
        for kc in range(8):
            self.DMA("sp", self.hx[:, kc, :], self.xT[bi, kc * 128:(kc + 1) * 128, :], [],
                     [self.Rh[(ci, kc)] for ci in range(4)])
        self.DMA("sp", self.hc[:], self.cT[bi].rearrange("(kc p) n -> p kc n", p=128), [],
                 [self.Rh[(4, kc)] for kc in range(8)])

    def store_h(self, bi):
        for kc in range(8):
            self.DMA("sp", self.yT[bi, kc * 128:(kc + 1) * 128, :], self.hx[:, kc, :],
                     [self.Rh[(ci, kc)] for ci in range(4)], [])

    def norm_sq(self, bi, ci, sqb):
        s, t0, n, col = CHUNKS[ci]
        for kc in range(8):
            sq, Rsq = sqb[kc]
            self.E("act", "activation", [self.Rh[(ci, kc)]], [Rsq], out=sq[:, :n], in_=self.hview(ci, kc),
                   func=AF.Square)

    def norm_rest(self, l, k, bi, ci, sqb):
        epsap = self.gp[:, OFF_EPS:OFF_EPS + 1]
        s, t0, n, col = CHUNKS[ci]
        v = bi if s == "x" else 2
        pss, Rpss = self.psum()
        for kc in range(8):
            sq, Rsq = sqb[kc]
            self.E("pe", "matmul", [Rsq, self.Rones], [Rpss], pss[:, :n], lhsT=self.ones[:], rhs=sq[:, :n],
                   start=(kc == 0), stop=(kc == 7))
        rstd, Rrstd = self.TR[self.tr_i], self.RTR[self.tr_i]
        self.tr_i = 1 - self.tr_i
        self.E("act", "activation", [Rpss, self.Rgp], [Rrstd], out=rstd[:, :n], in_=pss[:, :n], func=AF.Sqrt,
               scale=1.0 / D, bias=epsap)
        self.E("dve", "reciprocal", [Rrstd], [Rrstd], out=rstd[:, :n], in_=rstd[:, :n])
        for kc in range(8):
            tmp, Rtmp = self.tf()
            self.E("dve", "scalar_tensor_tensor", [self.Rh[(ci, kc)], self.RDRV[l], Rrstd], [Rtmp],
                   out=tmp[:, :n], in0=self.hview(ci, kc), scalar=self.DRV[:, l, k, kc, v:v + 1],
                   in1=rstd[:, :n], op0=ALU.mult, op1=ALU.mult)
            self.E("act", "activation", [Rtmp, self.RM[l]], [self.RnT[(ci, kc)]],
                   out=self.nT[:, kc, col:col + n], in_=tmp[:, :n], func=AF.Identity,
                   bias=self.M[:, l, (3 * k) * 8 + kc, v:v + 1], scale=1.0)

    def norm(self, l, k, bi, cis):
        for ci in cis:
            if ci in self.pre:
                continue
            s, t0, n, col = CHUNKS[ci]
            pss, Rpss = self.psum()
            for kc in range(8):
                sq, Rsq = self.tb()
                if kc % 2 == 0:
                    self.E("act", "activation", [self.Rh[(ci, kc)]], [Rsq], out=sq[:, :n], in_=self.hview(ci, kc),
                           func=AF.Square)
                else:
                    self.E("dve", "tensor_tensor", [self.Rh[(ci, kc)]], [Rsq], out=sq[:, :n],
                           in0=self.hview(ci, kc), in1=self.hview(ci, kc), op=ALU.mult)
                self.E("pe", "matmul", [Rsq, self.Rones], [Rpss], pss[:, :n], lhsT=self.ones[:], rhs=sq[:, :n],
                       start=(kc == 0), stop=(kc == 7))
            self._norm_tail(l, k, bi, ci, pss, Rpss)

    def _norm_tail(self, l, k, bi, ci, pss, Rpss):
        epsap = self.gp[:, OFF_EPS:OFF_EPS + 1]
        s, t0, n, col = CHUNKS[ci]
        v = bi if s == "x" else 2
        rstd, Rrstd = self.TR[self.tr_i], self.RTR[self.tr_i]
        self.tr_i = 1 - self.tr_i
        self.E("act", "activation", [Rpss, self.Rgp], [Rrstd], out=rstd[:, :n], in_=pss[:, :n], func=AF.Sqrt,
               scale=1.0 / D, bias=epsap)
        self.E("dve", "reciprocal", [Rrstd], [Rrstd], out=rstd[:, :n], in_=rstd[:, :n])
        for kc in range(8):
            tmp, Rtmp = self.tf()
            self.E("dve", "scalar_tensor_tensor", [self.Rh[(ci, kc)], self.RDRV[l], Rrstd], [Rtmp],
                   out=tmp[:, :n], in0=self.hview(ci, kc), scalar=self.DRV[:, l, k, kc, v:v + 1],
                   in1=rstd[:, :n], op0=ALU.mult, op1=ALU.mult)
            self.E("act", "activation", [Rtmp, self.RM[l]], [self.RnT[(ci, kc)]],
                   out=self.nT[:, kc, col:col + n], in_=tmp[:, :n], func=AF.Identity,
                   bias=self.M[:, l, (3 * k) * 8 + kc, v:v + 1], scale=1.0)

    def alloc_nsq(self):
        self.nsq = [(self.salloc(512, BF16), Res(f"nsq{i}", True)) for i in range(8)]

    def cb_final(self, bi, ci):
        if self.nxt is None:
            return
        l2, k2, cis2 = self.nxt
        if self.pending is not None:
            self.norm_rest(l2, k2, bi, self.pending, self.nsq)
            self.pending = None
        if ci in cis2:
            self.norm_sq(bi, ci, self.nsq)
            self.pending = ci
            self.pre_next.add(ci)

    def cb_flush(self, bi):
        if self.nxt is not None and self.pending is not None:
            l2, k2, cis2 = self.nxt
            self.norm_rest(l2, k2, bi, self.pending, self.nsq)
            self.pending = None

    def ffn(self, l, f, bi, cis):
        k = 0 if f == 0 else 2
        self.norm(l, k, bi, cis)
        if self.dbg and l == 0 and bi == 0 and f == 0:
            self.DMA("sp", self.dbgM[:, 0:216], self.M[:, 0].rearrange("p b c -> p (b c)"), list(self.RM), [])
            self.DMA("pool", self.dbgN, self.nT[:].rearrange("p a b -> p (a b)"), list(self.RnT.values()), [])
        self.fence()
        GL = 8
        act = self.salloc(GL * NTOK, BF16).rearrange("p (a b) -> p a b", a=GL)
        self.alloc_nsq()
        Ract = {(a, ci): Res(f"act{a}_{ci}", True) for a in range(GL) for ci in range(5)}
        if bi == 0 and f == 0 and l + 1 < self.n_layers:
            self.mods_begin(l + 1)
        w1v = self.w1[l, f].rearrange("(kc p) n -> p kc n", p=128)
        w3v = self.w3[l, f].rearrange("(kc p) n -> p kc n", p=128)
        w2v = self.w2[l, f].rearrange("(fc p) n -> p fc n", p=128)
        gk = 3 if f == 0 else 4
        def get_pair(p0):
            return (self.wget(w1v[:, :, p0 * 128:(p0 + 2) * 128], [128, 8, 256]),
                    self.wget(w3v[:, :, p0 * 128:(p0 + 2) * 128], [128, 8, 256]))

        cur = None
        for gi, (g0, g1) in enumerate(FF_GROUPS):
            pairs = list(range(g0, g1, 2))
            if cur is None:
                cur = get_pair(pairs[0])
            us = []
            for pidx, p0 in enumerate(pairs):
                if pidx + 1 < len(pairs):
                    nxt = get_pair(pairs[pidx + 1])
                else:
                    nxt = None
                    for q0 in pairs:
                        us.append(self.wget(w2v[:, q0:q0 + 2, :], [128, 2, 1024]))
                (u1, R1), (u3, R3) = cur
                cur = nxt
                for fc in range(p0, p0 + 2):
                    a = fc - g0
                    c0 = (fc - p0) * 128
                    for ci in cis:
                        s, t0, n, col = CHUNKS[ci]
                        p1, Rp1 = self.psum()
                        p3, Rp3 = self.psum()
                        for kc in range(8):
                            self.E("pe", "matmul", [R1, self.RnT[(ci, kc)]], [Rp1], p1[:, :n],
                                   lhsT=u1[:, kc, c0:c0 + 128], rhs=self.nT[:, kc, col:col + n],
                                   start=(kc == 0), stop=(kc == 7))
                        for kc in range(8):
                            self.E("pe", "matmul", [R3, self.RnT[(ci, kc)]], [Rp3], p3[:, :n],
                                   lhsT=u3[:, kc, c0:c0 + 128], rhs=self.nT[:, kc, col:col + n],
                                   start=(kc == 0), stop=(kc == 7))
                        sl, Rsl = self.tf()
                        self.E("act", "activation", [Rp1], [Rsl], out=sl[:, :n], in_=p1[:, :n], func=AF.Silu)
                        self.E("dve", "tensor_tensor", [Rsl, Rp3], [Ract[(a, ci)]], out=act[:, a, col:col + n],
                               in0=sl[:, :n], in1=p3[:, :n], op=ALU.mult)
                        self.mods_tick()
            if gi + 1 < len(FF_GROUPS):
                cur = get_pair(FF_GROUPS[gi + 1][0])
            for ci in cis:
                s, t0, n, col = CHUNKS[ci]
                v = bi if s == "x" else 2
                for d in range(8):
                    po, Rpo = self.psum()
                    for fc in range(g0, g1):
                        a = fc - g0
                        u2, R2 = us[a // 2]
                        self.E("pe", "matmul", [R2, Ract[(a, ci)]], [Rpo], po[:, :n],
                               lhsT=u2[:, a % 2, d * 128:(d + 1) * 128], rhs=act[:, a, col:col + n],
                               start=(fc == g0), stop=(fc == g1 - 1))
                    hv = self.hview(ci, d)
                    self.E("dve", "scalar_tensor_tensor", [Rpo, self.RDRV[l], self.Rh[(ci, d)]], [self.Rh[(ci, d)]],
                           out=hv, in0=po[:, :n], scalar=self.DRV[:, l, gk, d, v:v + 1], in1=hv,
                           op0=ALU.mult, op1=ALU.add)
                if (g0, g1) == FF_GROUPS[-1]:
                    self.cb_final(bi, ci)
        while self.mq is not None:
            self.mods_tick()

    def conv_mixer(self, l, bi, cis):
        j = l // 2
        self.norm(l, 1, bi, cis)
        self.fence()
        yb = self.salloc(2 * NTOK, BF16).rearrange("p (a b) -> p a b", a=2)
        Ry = {(a, ci): Res(f"y{a}_{ci}", True) for a in range(2) for ci in range(5)}
        bbuf = [self.salloc(NTOK, BF16) for _ in range(2)]
        Rb = [Res("bb0", True), Res("bb1", True)]
        self.alloc_nsq()
        VL = NTOK + 4
        vbuf = [self.salloc(VL, F32) for _ in range(2)]
        Rv = [Res("vb0", True), Res("vb1", True)]
        voff = lambda s: 1 if s == "x" else NX + 3
        for i in range(2):
            self.E("dve", "memset", [], [Rv[i]], vbuf[i][:, :], 0.0)
        wiv = self.sc_w_in[j].rearrange("(kc p) n -> p kc n", p=128)
        wov = self.sc_w_out[j].rearrange("(cc p) n -> p cc n", p=128)
        cvw = lambda tap, ch: self.gp[:, OFF_CV + (j * 3 + tap) * 8 + ch: OFF_CV + (j * 3 + tap) * 8 + ch + 1]
        it = 0
        for cp in range(4):
            ub, Rub = self.wget(wiv[:, :, cp * 256:(cp + 1) * 256], [128, 8, 256])
            uc, Ruc = self.wget(wiv[:, :, D + cp * 256:D + (cp + 1) * 256], [128, 8, 256])
            uu, Ruu = self.wget(wiv[:, :, 2 * D + cp * 256:2 * D + (cp + 1) * 256], [128, 8, 256])
            for cc in range(2):
                ch = cp * 2 + cc
                c0 = cc * 128
                bb, Rbb = bbuf[it % 2], Rb[it % 2]
                vb, Rvb = vbuf[it % 2], Rv[it % 2]
                it += 1
                for ci in cis:
                    s, t0, n, col = CHUNKS[ci]
                    pb, Rpb = self.psum()
                    pc, Rpc = self.psum()
                    pu, Rpu = self.psum()
                    for (pp, Rpp, uw, Ruw) in ((pb, Rpb, ub, Rub), (pc, Rpc, uc, Ruc), (pu, Rpu, uu, Ruu)):
                        for kc in range(8):
                            self.E("pe", "matmul", [Ruw, self.RnT[(ci, kc)]], [Rpp], pp[:, :n],
                                   lhsT=uw[:, kc, c0:c0 + 128], rhs=self.nT[:, kc, col:col + n],
                                   start=(kc == 0), stop=(kc == 7))
                    self.E("act", "activation", [Rpb], [Rbb], out=bb[:, col:col + n], in_=pb[:, :n], func=AF.Identity)
                    cs, Rcs = self.tf()
                    self.E("act", "activation", [Rpc], [Rcs], out=cs[:, :n], in_=pc[:, :n], func=AF.Identity)
                    vo = voff(s) + t0
                    self.E("dve", "tensor_tensor", [Rcs, Rpu], [Rvb], out=vb[:, vo:vo + n], in0=cs[:, :n],
                           in1=pu[:, :n], op=ALU.mult)
                for ci in cis:
                    s, t0, n, col = CHUNKS[ci]
                    vo = voff(s) + t0
                    a1, Ra1 = self.tf()
                    self.E("act", "activation", [Rvb, self.Rgp], [Ra1], out=a1[:, :n], in_=vb[:, vo:vo + n],
                           func=AF.Identity, scale=cvw(1, ch))
                    self.E("dve", "scalar_tensor_tensor", [Rvb, self.Rgp, Ra1], [Ra1], out=a1[:, :n],
                           in0=vb[:, vo - 1:vo - 1 + n], scalar=cvw(0, ch), in1=a1[:, :n], op0=ALU.mult, op1=ALU.add)
                    self.E("dve", "scalar_tensor_tensor", [Rvb, self.Rgp, Ra1], [Ra1], out=a1[:, :n],
                           in0=vb[:, vo + 1:vo + 1 + n], scalar=cvw(2, ch), in1=a1[:, :n], op0=ALU.mult, op1=ALU.add)
                    self.E("dve", "tensor_tensor", [Ra1, Rbb], [Ry[(cc, ci)]], out=yb[:, cc, col:col + n],
                           in0=a1[:, :n], in1=bb[:, col:col + n], op=ALU.mult)
            uo, Ruo = self.wget(wov[:, cp * 2:cp * 2 + 2, :], [128, 2, 1024])
            for ci in cis:
                s, t0, n, col = CHUNKS[ci]
                v = bi if s == "x" else 2
                for d in range(8):
                    po, Rpo = self.psum()
                    for cc in range(2):
                        self.E("pe", "matmul", [Ruo, Ry[(cc, ci)]], [Rpo], po[:, :n],
                               lhsT=uo[:, cc, d * 128:(d + 1) * 128], rhs=yb[:, cc, col:col + n],
                               start=(cc == 0), stop=(cc == 1))
                    hv = self.hview(ci, d)
                    self.E("dve", "scalar_tensor_tensor", [Rpo, self.RM[l], self.Rh[(ci, d)]], [self.Rh[(ci, d)]],
                           out=hv, in0=po[:, :n], scalar=self.M[:, l, 5 * 8 + d, v:v + 1], in1=hv,
                           op0=ALU.mult, op1=ALU.add)
                if cp == 3:
                    self.cb_final(bi, ci)

    def rstd_from(self, parts, n, dim):
        pss, Rpss = self.psum()
        for i, (pa, Rpa, P) in enumerate(parts):
            sq, Rsq = self.tb()
            self.E("act", "activation", [Rpa], [Rsq], out=sq[0:P, :n], in_=pa, func=AF.Square)
            self.E("pe", "matmul", [Rsq, self.Rones], [Rpss], pss[:, :n], lhsT=self.ones[0:P, :], rhs=sq[0:P, :n],
                   start=(i == 0), stop=(i == len(parts) - 1))
        rstd, Rrstd = self.tf()
        self.E("act", "activation", [Rpss, self.Rgp], [Rrstd], out=rstd[:, :n], in_=pss[:, :n], func=AF.Sqrt,
               scale=1.0 / dim, bias=self.gp[:, OFF_EPS:OFF_EPS + 1])
        self.E("dve", "reciprocal", [Rrstd], [Rrstd], out=rstd[:, :n], in_=rstd[:, :n])
        return rstd, Rrstd

    def mla_mixer(self, l, bi, ctx_q):
        j = l // 2
        cis = [0, 1, 2, 3, 4]
        self.norm(l, 1, bi, cis)
        self.fence()
        cqn = self.salloc(2 * NTOK, BF16).rearrange("p (a b) -> p a b", a=2)
        ckvn = self.salloc(NTOK, BF16)
        krp = self.salloc(NTOK, BF16)
        ssr = self.salloc(32, F32)
        sck2 = [self.salloc(32, F32) for _ in range(2)]
        sck_t = self.salloc(32, F32)
        sqk = self.salloc(512, BF16)
        Rsqk = Res("sqk", True)
        Rssr = Res("ssr", True)
        Rsck2 = [Res("sck0", True), Res("sck1", True)]
        Rsckt = Res("sckt", True)
        tabC = self.salloc(NX, BF16)
        tabS = self.salloc(NX, BF16)
        Kn2 = [self.salloc(NTOK, BF16) for _ in range(2)]
        Vh2 = [self.salloc(18 * 128, BF16).rearrange("p (a b) -> p a b", a=18) for _ in range(2)]
        qn = [self.salloc(512, BF16) for _ in range(2)]
        qr = [self.salloc(512, BF16) for _ in range(2)]
        Rcqn = {ci: Res(f"cqn{ci}", True) for ci in cis}
        Rckvn = {ci: Res(f"ckvn{ci}", True) for ci in cis}
        Rkrp = {ci: Res(f"krp{ci}", True) for ci in cis}
        Rtab = Res("tab", True)
        RKn2 = [{ci: Res(f"Kn{b}_{ci}", True) for ci in cis} for b in range(2)]
        RV2 = [{t: Res(f"V{b}_{t}", True) for t in range(18)} for b in range(2)]
        Rq = [Res("q0", True), Res("q1", True)]
        gcol = lambda off, c: self.gp[:, off + c: off + c + 1]

        wav = self.w_a[j].rearrange("(kc p) n -> p kc n", p=128)
        ua0, Rua0 = self.wget(wav[:, :, 0:256], [128, 8, 256])
        ua1, Rua1 = self.wget(wav[:, :, 256:512], [128, 8, 256])
        self.DMA("pool", tabC[0:64, :], self.rope_d[:, 0:NX], [], [Rtab])
        self.DMA("pool", tabS[0:64, :], self.rope_d[:, NX:2 * NX], [], [Rtab])
        for ci in cis:
            s, t0, n, col = CHUNKS[ci]
            outs = []
            for (uw, Ruw, c0, P) in ((ua0, Rua0, 0, 128), (ua0, Rua0, 128, 128), (ua1, Rua1, 0, 128),
                                     (ua1, Rua1, 128, 64), (ua1, Rua1, 192, 64)):
                pp, Rpp = self.psum()
                for kc in range(8):
                    self.E("pe", "matmul", [Ruw, self.RnT[(ci, kc)]], [Rpp], pp[0:P, :n],
                           lhsT=uw[:, kc, c0:c0 + P], rhs=self.nT[:, kc, col:col + n],
                           start=(kc == 0), stop=(kc == 7))
                outs.append((pp, Rpp))
            (pq0, Rpq0), (pq1, Rpq1), (pkv, Rpkv), (pkr, Rpkr), (pks, Rpks) = outs
            rs, Rrs = self.rstd_from([(pq0[:, :n], Rpq0, 128), (pq1[:, :n], Rpq1, 128)], n, 256)
            for a, (pq, Rpq) in enumerate(((pq0, Rpq0), (pq1, Rpq1))):
                self.E("dve", "scalar_tensor_tensor", [Rpq, self.Rgp, Rrs], [Rcqn[ci]], out=cqn[:, a, col:col + n],
                       in0=pq[:, :n], scalar=gcol(OFF_GQA, j * 2 + a), in1=rs[:, :n], op0=ALU.mult, op1=ALU.mult)
            rs2, Rrs2 = self.rstd_from([(pkv[:, :n], Rpkv, 128)], n, 128)
            self.E("dve", "scalar_tensor_tensor", [Rpkv, self.Rgp, Rrs2], [Rckvn[ci]], out=ckvn[:, col:col + n],
                   in0=pkv[:, :n], scalar=gcol(OFF_GKVA, j), in1=rs2[:, :n], op0=ALU.mult, op1=ALU.mult)
            ksq, Rksq = self.tb()
            self.E("act", "activation", [Rpkr], [Rksq], out=ksq[0:64, :n], in_=pkr[0:64, :n], func=AF.Square)
            for tt in range(n // 128):
                kt = col // 128 + tt
                self.E("pe", "matmul", [Rksq, self.Rones], [self.Rps[7]], self.ps[7][:, kt:kt + 1],
                       lhsT=ksq[0:64, tt * 128:(tt + 1) * 128], rhs=self.ones[0:64, 0:1], start=True, stop=True)
            if s == "x":
                t1, Rt1 = self.tf()
                t2, Rt2 = self.tf()
                self.E("dve", "scalar_tensor_tensor", [Rpkr, self.Rgp, Rtab], [Rt1], out=t1[0:64, :n],
                       in0=pkr[0:64, :n], scalar=self.gp[0:64, OFF_GK + j * 3 + 1:OFF_GK + j * 3 + 2],
                       in1=tabC[0:64, t0:t0 + n], op0=ALU.mult, op1=ALU.mult)
                self.E("dve", "scalar_tensor_tensor", [Rpks, self.Rgp, Rtab], [Rt2], out=t2[0:64, :n],
                       in0=pks[0:64, :n], scalar=self.gp[0:64, OFF_GK + j * 3 + 2:OFF_GK + j * 3 + 3],
                       in1=tabS[0:64, t0:t0 + n], op0=ALU.mult, op1=ALU.mult)
                self.E("dve", "tensor_tensor", [Rt1, Rt2], [Rkrp[ci]], out=krp[0:64, col:col + n], in0=t1[0:64, :n],
                       in1=t2[0:64, :n], op=ALU.add)
            else:
                self.E("act", "activation", [Rpkr, self.Rgp], [Rkrp[ci]], out=krp[0:64, col:col + n],
                       in_=pkr[0:64, :n], func=AF.Identity,
                       scale=self.gp[0:64, OFF_GK + j * 3 + 1:OFF_GK + j * 3 + 2])

        self.E("dve", "tensor_copy", [self.Rps[7]], [Rssr], out=ssr[:, 0:18], in_=self.ps[7][:, 0:18])
        Ro = {(h, ci): self.RnT[(ci, h)] for h in range(H) for ci in cis}
        ukv, Rukv = self.wget(self.w_ukv[j], [128, 2048])
        uqn = [self.wget(self.w_uqn[j].rearrange("(kc p) n -> p kc n", p=128)[:, :, hh * 512:(hh + 1) * 512],
                         [128, 2, 512]) for hh in range(2)]
        uqr = [self.wget(self.w_uqr[j].rearrange("(kc p) n -> p kc n", p=128)[:, :, hh * 512:(hh + 1) * 512],
                         [128, 2, 512]) for hh in range(2)]
        SC = 192.0 ** -0.5
        qcis = cis if ctx_q else [0, 1, 2, 3]
        nq = len(qcis)
        sqn_b = self.salloc(512, BF16)
        sqr_b = self.salloc(512, BF16)
        Rsqq = Res("sqq", True)
        epsap = self.gp[:, OFF_EPS:OFF_EPS + 1]
        LOOK = 3
        psB = [3, 4]
        st = {"b": 0, "q": 0}
        qst = {}

        def psumB():
            i = psB[st["b"] % len(psB)]
            st["b"] += 1
            return self.ps[i], self.Rps[i]

        def k_s1(h, ci):
            b = h % 2
            s_, t0, n, col = CHUNKS[ci]
            pk, Rpk = psumB()
            self.E("pe", "matmul", [Rukv, Rckvn[ci]], [Rpk], pk[:, :n], lhsT=ukv[:, h * 256:h * 256 + 128],
                   rhs=ckvn[:, col:col + n], start=True, stop=True)
            self.E("act", "activation", [Rpk, self.Rgp], [RKn2[b][ci]], out=Kn2[b][:, col:col + n], in_=pk[:, :n],
                   func=AF.Identity, scale=gcol(OFF_GK, j * 3))
            self.E("act", "activation", [Rpk], [Rsqk], out=sqk[:, :n], in_=pk[:, :n], func=AF.Square)
            pv, Rpv = psumB()
            nt = n // 128
            for tt in range(nt):
                self.E("pe", "matmul", [Rukv, Rckvn[ci]], [Rpv], pv[:, tt * 128:(tt + 1) * 128],
                       lhsT=ckvn[:, col + tt * 128: col + (tt + 1) * 128],
                       rhs=ukv[:, h * 256 + 128:h * 256 + 256], start=True, stop=True)
            tk0 = col // 128
            self.E("act", "activation", [Rpv], [RV2[b][tk0 + tt] for tt in range(nt)],
                   out=Vh2[b][:, tk0:tk0 + nt, :], in_=pv[:, :n].rearrange("p (a b) -> p a b", a=nt),
                   func=AF.Identity)

        def k_s2(h, ci):
            b = h % 2
            s_, t0, n, col = CHUNKS[ci]
            tk0 = col // 128
            for tt in range(n // 128):
                c = b * 32 + tk0 + tt
                self.E("pe", "matmul", [Rsqk, self.Rones], [self.Rps[5]], self.ps[5][:, c:c + 1],
                       lhsT=sqk[:, tt * 128:(tt + 1) * 128], rhs=self.ones[:, 0:1], start=True, stop=True)

        def k_fin(h):
            b = h % 2
            self.E("dve", "tensor_tensor", [self.Rps[5], Rssr], [Rsckt], out=sck_t[:, 0:18],
                   in0=self.ps[5][:, b * 32:b * 32 + 18], in1=ssr[:, 0:18], op=ALU.add)
            self.E("act", "activation", [Rsckt, self.Rgp], [Rsckt], out=sck_t[:, 0:18], in_=sck_t[:, 0:18],
                   func=AF.Sqrt, scale=1.0 / 192, bias=epsap)
            self.E("dve", "reciprocal", [Rsckt], [Rsckt], out=sck_t[:, 0:18], in_=sck_t[:, 0:18])
            self.E("dve", "tensor_scalar", [Rsckt], [Rsck2[b]], out=sck2[b][:, 0:18], in0=sck_t[:, 0:18], scalar1=SC,
                   scalar2=None, op0=ALU.mult)

        def q_sA(h, qi):
            ci = qcis[qi]
            s_, t0, n, col = CHUNKS[ci]
            uqn_h, Ruqn_h = uqn[h // 4]
            uqr_h, Ruqr_h = uqr[h // 4]
            hh = h % 4
            pqn, Rpqn = self.ps[0], self.Rps[0]
            pqr, Rpqr = self.ps[1], self.Rps[1]
            pqs, Rpqs = self.ps[2], self.Rps[2]
            for kc in range(2):
                self.E("pe", "matmul", [Ruqn_h, Rcqn[ci]], [Rpqn], pqn[:, :n],
                       lhsT=uqn_h[:, kc, hh * 128:(hh + 1) * 128], rhs=cqn[:, kc, col:col + n],
                       start=(kc == 0), stop=(kc == 1))
            for kc in range(2):
                self.E("pe", "matmul", [Ruqr_h, Rcqn[ci]], [Rpqr], pqr[0:64, :n],
                       lhsT=uqr_h[:, kc, hh * 128:hh * 128 + 64], rhs=cqn[:, kc, col:col + n],
                       start=(kc == 0), stop=(kc == 1))
            for kc in range(2):
                self.E("pe", "matmul", [Ruqr_h, Rcqn[ci]], [Rpqs], pqs[0:64, :n],
                       lhsT=uqr_h[:, kc, hh * 128 + 64:hh * 128 + 128], rhs=cqn[:, kc, col:col + n],
                       start=(kc == 0), stop=(kc == 1))
            self.E("act", "activation", [Rpqn], [Rsqq], out=sqn_b[:, :n], in_=pqn[:, :n], func=AF.Square)
            self.E("act", "activation", [Rpqr], [Rsqq], out=sqr_b[0:64, :n], in_=pqr[0:64, :n], func=AF.Square)

        def q_sB(h, qi):
            ci = qcis[qi]
            s_, t0, n, col = CHUNKS[ci]
            pqn, Rpqn = self.ps[0], self.Rps[0]
            pqr, Rpqr = self.ps[1], self.Rps[1]
            pqs, Rpqs = self.ps[2], self.Rps[2]
            qb = st["q"] % 2
            st["q"] += 1
            qst[(h, qi)] = qb
            pss, Rpss = psumB()
            self.E("pe", "matmul", [Rsqq, self.Rones], [Rpss], pss[:, :n], lhsT=self.ones[:], rhs=sqn_b[:, :n],
                   start=True, stop=False)
            self.E("pe", "matmul", [Rsqq, self.Rones], [Rpss], pss[:, :n], lhsT=self.ones[0:64, :],
                   rhs=sqr_b[0:64, :n], start=False, stop=True)
            rs, Rrs = self.tf()
            self.E("act", "activation", [Rpss, self.Rgp], [Rrs], out=rs[:, :n], in_=pss[:, :n], func=AF.Sqrt,
                   scale=1.0 / 192, bias=epsap)
            self.E("dve", "reciprocal", [Rrs], [Rrs], out=rs[:, :n], in_=rs[:, :n])
            self.E("dve", "scalar_tensor_tensor", [Rpqn, self.Rgp, Rrs], [Rq[qb]], out=qn[qb][:, :n],
                   in0=pqn[:, :n], scalar=gcol(OFF_GQ, j * 3), in1=rs[:, :n], op0=ALU.mult, op1=ALU.mult)
            if s_ == "x":
                t1, Rt1 = self.tf()
                t2, Rt2 = self.tf()
                self.E("dve", "scalar_tensor_tensor", [Rpqr, self.Rgp, Rtab], [Rt1], out=t1[0:64, :n],
                       in0=pqr[0:64, :n], scalar=self.gp[0:64, OFF_GQ + j * 3 + 1:OFF_GQ + j * 3 + 2],
                       in1=tabC[0:64, t0:t0 + n], op0=ALU.mult, op1=ALU.mult)
                self.E("dve", "scalar_tensor_tensor", [Rpqs, self.Rgp, Rtab], [Rt2], out=t2[0:64, :n],
                       in0=pqs[0:64, :n], scalar=self.gp[0:64, OFF_GQ + j * 3 + 2:OFF_GQ + j * 3 + 3],
                       in1=tabS[0:64, t0:t0 + n], op0=ALU.mult, op1=ALU.mult)
                self.E("dve", "tensor_tensor", [Rt1, Rt2], [Rt1], out=t1[0:64, :n], in0=t1[0:64, :n],
                       in1=t2[0:64, :n], op=ALU.add)
                self.E("dve", "tensor_tensor", [Rt1, Rrs], [Rq[qb]], out=qr[qb][0:64, :n], in0=t1[0:64, :n],
                       in1=rs[0:64, :n], op=ALU.mult)
            else:
                self.E("dve", "scalar_tensor_tensor", [Rpqr, self.Rgp, Rrs], [Rq[qb]], out=qr[qb][0:64, :n],
                       in0=pqr[0:64, :n], scalar=self.gp[0:64, OFF_GQ + j * 3 + 1:OFF_GQ + j * 3 + 2],
                       in1=rs[0:64, :n], op0=ALU.mult, op1=ALU.mult)

        def attention(h, qi, hooks):
            b = h % 2
            ci = qcis[qi]
            s_, t0, n, col = CHUNKS[ci]
            qb = qst[(h, qi)]
            kts = list(range(18)) if s_ == "x" else [16, 17]
            nk = len(kts)
            po, Rpo = self.ps[6], self.Rps[6]
            pd, Rpd = self.ps[7], self.Rps[7]
            pts = {}
            hooks = sorted(hooks, key=lambda x: x[0])
            hi = 0
            for step in range(nk + LOOK):
                if step < nk:
                    kt = kts[step]
                    kci = min(kt // 4, 4)
                    psc, Rpsc = psumB()
                    self.E("pe", "matmul", [RKn2[b][kci], Rq[qb]], [Rpsc], psc[:, :n],
                           lhsT=Kn2[b][:, kt * 128:(kt + 1) * 128], rhs=qn[qb][:, :n], start=True, stop=False)
                    self.E("pe", "matmul", [Rkrp[kci], Rq[qb]], [Rpsc], psc[:, :n],
                           lhsT=krp[0:64, kt * 128:(kt + 1) * 128], rhs=qr[qb][0:64, :n], start=False, stop=True)
                    pt, Rpt = self.tb()
                    self.E("act", "activation", [Rpsc, Rsck2[b]], [Rpt], out=pt[:, :n], in_=psc[:, :n], func=AF.Exp,
                           scale=sck2[b][:, kt:kt + 1])
                    pts[step] = (pt, Rpt)
                while hi < len(hooks) and hooks[hi][0] <= step:
                    hooks[hi][1]()
                    hi += 1
                if step >= LOOK:
                    ki = step - LOOK
                    kt = kts[ki]
                    pt, Rpt = pts.pop(ki)
                    self.E("pe", "matmul", [RV2[b][kt], Rpt], [Rpo], po[:, :n], lhsT=Vh2[b][:, kt, :], rhs=pt[:, :n],
                           start=(ki == 0), stop=(ki == nk - 1))
                    self.E("pe", "matmul", [self.Rones, Rpt], [Rpd], pd[:, :n], lhsT=self.ones[:], rhs=pt[:, :n],
                           start=(ki == 0), stop=(ki == nk - 1))
            while hi < len(hooks):
                hooks[hi][1]()
                hi += 1
            rd, Rrd = self.tf()
            self.E("dve", "reciprocal", [Rpd], [Rrd], out=rd[:, :n], in_=pd[:, :n])
            self.E("dve", "tensor_tensor", [Rpo, Rrd], [Ro[(h, ci)]], out=self.nT[:, h, col:col + n],
                   in0=po[:, :n], in1=rd[:, :n], op=ALU.mult)

        for ci in cis:
            k_s1(0, ci)
            k_s2(0, ci)
        k_fin(0)
        q_sA(0, 0)
        q_sB(0, 0)
        items = [(h, qi) for h in range(H) for qi in range(nq)]
        kassign = {0: [0], 1: [1], 2: [2], 3: [3, 4], 4: []}
        for idx, (h, qi) in enumerate(items):
            hooks = []
            if idx + 1 < len(items):
                h2, q2 = items[idx + 1]
                hooks.append((1, (lambda a=h2, c=q2: q_sA(a, c))))
                hooks.append((6, (lambda a=h2, c=q2: q_sB(a, c))))
            if h + 1 < H:
                stp = 3
                for kc_ in kassign[qi]:
                    hooks.append((stp, (lambda a=h + 1, c=kc_: k_s1(a, c))))
                    hooks.append((stp + 5, (lambda a=h + 1, c=kc_: k_s2(a, c))))
                    stp += 8
                if qi == 3:
                    hooks.append((19, (lambda a=h + 1: k_fin(a))))
            if idx == len(items) - 1:
                wov = self.w_o[j].rearrange("(hh p) n -> p hh n", p=128)
                uos = [self.wget(wov[:, :, dp * 256:(dp + 1) * 256], [128, 8, 256]) for dp in range(4)]
            attention(h, qi, hooks)

        self.fence()
        self.alloc_nsq()
        for ci in qcis:
            s, t0, n, col = CHUNKS[ci]
            v = bi if s == "x" else 2
            for d in range(8):
                uo, Ruo = uos[d // 2]
                dd = d % 2
                po, Rpo = self.psum()
                for h in range(H):
                    self.E("pe", "matmul", [Ruo, Ro[(h, ci)]], [Rpo], po[:, :n],
                           lhsT=uo[:, h, dd * 128:(dd + 1) * 128], rhs=self.nT[:, h, col:col + n],
                           start=(h == 0), stop=(h == H - 1))
                hv = self.hview(ci, d)
                self.E("dve", "scalar_tensor_tensor", [Rpo, self.RM[l], self.Rh[(ci, d)]], [self.Rh[(ci, d)]],
                       out=hv, in0=po[:, :n], scalar=self.M[:, l, 5 * 8 + d, v:v + 1], in1=hv,
                       op0=ALU.mult, op1=ALU.add)
            self.cb_final(bi, ci)

    def run_batch(self, bi):
        parts = os.environ.get("KTEST_PARTS", "f1,mix,f2").split(",")
        overlap = os.environ.get("KTEST_NOOVERLAP", "") == ""
        phases = []
        xs = [0, 1, 2, 3]
        for l in range(self.n_layers):
            kind = l % 2
            last = (l == DEPTH - 1)
            rin = (not last) or kind == 1
            rout = not last
            if "f1" in parts:
                phases.append(("ffn", l, 0, xs + ([4] if rin else []), 0))
            if "mix" in parts:
                if kind == 0:
                    phases.append(("conv", l, None, xs + ([4] if rout else []), 1))
                else:
                    phases.append(("mla", l, rout, [0, 1, 2, 3, 4], 1))
            if "f2" in parts:
                phases.append(("ffn", l, 1, xs + ([4] if rout else []), 2))
        self.pre = set()
        for i, (kind, l, arg, cis, k) in enumerate(phases):
            self.nxt = None
            if overlap and i + 1 < len(phases):
                n_ = phases[i + 1]
                self.nxt = (n_[1], n_[4], set(n_[3]))
            self.pre_next = set()
            self.pending = None
            if kind == "ffn":
                self.ffn(l, arg, bi, cis)
            elif kind == "conv":
                self.conv_mixer(l, bi, cis)
            else:
                self.mla_mixer(l, bi, arg)
            self.cb_flush(bi)
            self.pre = self.pre_next


def _rope_tables():
    t = np.arange(NX)
    r = (t // 64).astype(np.float32)
    c = (t % 64).astype(np.float32)
    inv = (10000.0 ** (-np.arange(16, dtype=np.float32) / 16)).astype(np.float32)
    ang = [r[None, :] * inv[:, None], c[None, :] * inv[:, None]]
    C = np.zeros((64, NX), np.float32)
    S = np.zeros((64, NX), np.float32)
    for a in range(2):
        for half in range(2):
            rows = slice(a * 32 + half * 16, a * 32 + half * 16 + 16)
            C[rows] = np.cos(ang[a])
            S[rows] = np.sin(ang[a]) * (-1.0 if half == 0 else 1.0)
    return np.concatenate([C, S], axis=1).astype(np.float32)


def _swap_idx():
    d = np.arange(64)
    a, half, f = d // 32, (d % 32) // 16, d % 16
    return a * 32 + (1 - half) * 16 + f


_CACHE = {}


def kernel(x, c, ctx, c_ctx, w_mod, b_mod, g_norm, ffn_w1, ffn_w3, ffn_w2, sc_w_in, sc_conv, sc_w_out,
           mla_w_a, mla_g_qa, mla_w_uq, mla_g_kva, mla_w_ukv, mla_g_q, mla_g_k, mla_w_o):
    n_layers = int(os.environ.get("KTEST_LAYERS", DEPTH))
    n_cores = int(os.environ.get("KTEST_CORES", 8))
    nb = 2
    f = lambda a: np.ascontiguousarray(np.asarray(a, dtype=np.float32))
    x, c, ctx, c_ctx = f(x), f(c), f(ctx), f(c_ctx)
    sw = _swap_idx()
    gp = np.zeros((128, NG), np.float32)
    gn = f(g_norm)
    for l in range(DEPTH):
        for k in range(3):
            gp[:, OFF_GN + (l * 3 + k) * 8: OFF_GN + (l * 3 + k + 1) * 8] = gn[l, k].reshape(8, 128).T
        gp[:, OFF_BM + l * 72: OFF_BM + (l + 1) * 72] = f(b_mod)[l].reshape(72, 128).T
    for j in range(2):
        for tap in range(3):
            gp[:, OFF_CV + (j * 3 + tap) * 8: OFF_CV + (j * 3 + tap + 1) * 8] = f(sc_conv)[j, tap].reshape(8, 128).T
        gp[:, OFF_GQA + j * 2: OFF_GQA + j * 2 + 2] = f(mla_g_qa)[j].reshape(2, 128).T
        gp[:, OFF_GKVA + j] = f(mla_g_kva)[j]
        for off, g in ((OFF_GQ, f(mla_g_q)[j]), (OFF_GK, f(mla_g_k)[j])):
            gp[:, off + j * 3] = g[:128]
            gp[:64, off + j * 3 + 1] = g[128:]
            gp[:64, off + j * 3 + 2] = g[128:][sw]
    gp[:, OFF_EPS] = EPS
    wa = f(mla_w_a)
    w_a = np.ascontiguousarray(np.concatenate([wa, wa[:, :, 384:448][:, :, sw]], axis=2))
    wuq = f(mla_w_uq).reshape(2, 256, H, 192)
    w_uqn = np.ascontiguousarray(wuq[:, :, :, :128].reshape(2, 256, H * 128))
    w_uqr = np.ascontiguousarray(np.concatenate([wuq[:, :, :, 128:], wuq[:, :, :, 128:][:, :, :, sw]], axis=3).reshape(2, 256, H * 128))
    rope = _rope_tables()
    shared = {
        "gpack": gp, "rope": rope, "w_mod": f(w_mod), "ffn_w1": f(ffn_w1), "ffn_w3": f(ffn_w3), "ffn_w2": f(ffn_w2),
        "sc_w_in": f(sc_w_in), "sc_w_out": f(sc_w_out), "w_a": w_a, "w_uqn": w_uqn, "w_uqr": w_uqr,
        "w_ukv": f(mla_w_ukv), "w_o": f(mla_w_o),
    }
    in_maps = []
    for i in range(n_cores):
        bs = slice(i * nb, (i + 1) * nb)
        m = dict(shared)
        m["xT"] = np.ascontiguousarray(x[bs].transpose(0, 2, 1))
        m["cT"] = np.ascontiguousarray(ctx[bs].transpose(0, 2, 1))
        m["condT"] = np.ascontiguousarray(np.stack([c[i * nb], c[i * nb + 1], c_ctx], axis=1))
        in_maps.append(m)
    key = (n_layers, nb)
    if key not in _CACHE:
        _CACHE[key] = Builder(n_layers, nb).build()
    nc = _CACHE[key]
    res = run_bass_kernel_spmd(nc, in_maps, core_ids=list(range(n_cores)))
    out = np.concatenate([r["yT"].transpose(0, 2, 1) for r in res.results], axis=0)
    if os.environ.get("KTEST_DBG", ""):
        _CACHE["dbg"] = res.results[0]
    return np.ascontiguousarray(out.astype(np.float32))
```
